# Optimizing a Trainium2 kernel written in Bass

```python
import math
import jax, jax.numpy as jnp
from jax import lax
import numpy as np

D_MODEL = 1024
BATCH = 8
SEQ = 2048
DEPTH = 2
DEC_BATCH = 128
DEC_SEQ = 4
PAST_LEN = 16384
PAGE_SIZE = 128

LRU_WIDTH = D_MODEL // 2
LRU_HEADS = 8
LRU_HEAD_DIM = LRU_WIDTH // LRU_HEADS
LRU_CONV = 4
LRU_C = 8.0
SC_WIDTH = D_MODEL // 2
SC_CONV = 3
SSM_WIDTH = D_MODEL // 2
SSM_GROUP = 16
SSM_GROUPS = SSM_WIDTH // SSM_GROUP
SSM_STATE = 64
N_BRANCH = 3
D_FF = 4 * D_MODEL
N_IN = 2 * LRU_WIDTH + 3 * SC_WIDTH + SSM_WIDTH + N_BRANCH * D_MODEL
SPLITS = [LRU_WIDTH, 2 * LRU_WIDTH, 2 * LRU_WIDTH + SC_WIDTH, 2 * LRU_WIDTH + 2 * SC_WIDTH,
          2 * LRU_WIDTH + 3 * SC_WIDTH, 2 * LRU_WIDTH + 3 * SC_WIDTH + SSM_WIDTH]
ALPHA = (2 * DEPTH) ** 0.25
BETA = (8 * DEPTH) ** -0.25
LN_EPS = 1e-5

kernel_name = "hybrid_rglru_shortconv_s5_step"

F32 = jnp.float32


def layer_norm(x, g, b):
    xf = x.astype(F32)
    mu = jnp.mean(xf, axis=-1, keepdims=True)
    var = jnp.mean(jnp.square(xf - mu), axis=-1, keepdims=True)
    y = (xf - mu) * lax.rsqrt(var + LN_EPS) * g.astype(F32) + b.astype(F32)
    return y.astype(x.dtype)


def causal_conv(u, buf, w):
    K = w.shape[0]
    L = u.shape[1]
    up = jnp.concatenate([buf.astype(u.dtype), u], axis=1)
    out = up[:, 0:L] * w[0]
    for k in range(1, K):
        out = out + up[:, k:k + L] * w[k]
    return out, up[:, -(K - 1):]


def real_linear_scan(a, b, h0):
    def comb(e1, e2):
        a1, b1 = e1
        a2, b2 = e2
        return a1 * a2, a2 * b1 + b2
    A, B = lax.associative_scan(comb, (a, b), axis=1)
    h = A * h0[:, None] + B
    return h, h[:, -1]


def complex_linear_scan(a_re, a_im, b_re, b_im, h0_re, h0_im):
    def comb(e1, e2):
        ar1, ai1, br1, bi1 = e1
        ar2, ai2, br2, bi2 = e2
        return (ar2 * ar1 - ai2 * ai1, ar2 * ai1 + ai2 * ar1,
                ar2 * br1 - ai2 * bi1 + br2, ar2 * bi1 + ai2 * br1 + bi2)
    Ar, Ai, Br, Bi = lax.associative_scan(comb, (a_re, a_im, b_re, b_im), axis=1)
    h0r = h0_re[:, None]
    h0i = h0_im[:, None]
    h_re = Ar * h0r - Ai * h0i + Br
    h_im = Ar * h0i + Ai * h0r + Bi
    return h_re, h_im


def mixer(x, lru_conv_buf, lru_h, sc_buf, ssm_re, ssm_im, p):
    bsz, L, _ = x.shape
    dt_ = x.dtype
    z = x @ p["w_in"]
    xa, ya, sb, sc, sh, us, gl = jnp.split(z, SPLITS, axis=-1)

    xa_c, new_lru_conv = causal_conv(xa, lru_conv_buf, p["conv_a_w"])
    xa_c = xa_c + p["conv_a_b"]
    xh = xa_c.reshape(bsz, L, LRU_HEADS, LRU_HEAD_DIM)
    gx = jax.nn.sigmoid(jnp.einsum('blhi,hij->blhj', xh, p["gate_x_w"]).reshape(bsz, L, LRU_WIDTH) + p["gate_x_b"])
    ga = jax.nn.sigmoid(jnp.einsum('blhi,hij->blhj', xh, p["gate_a_w"]).reshape(bsz, L, LRU_WIDTH) + p["gate_a_b"])
    log_a = -LRU_C * ga.astype(F32) * jax.nn.softplus(-p["lru_lambda"].astype(F32))
    a = jnp.exp(log_a)
    bx = jnp.sqrt(-jnp.expm1(2.0 * log_a)) * (gx * xa_c).astype(F32)
    h, h_last = real_linear_scan(a, bx, lru_h.astype(F32))
    out_a = (h.astype(dt_) * jax.nn.gelu(ya)) @ p["proj_a"]

    u = sc * sh
    cu, new_sc = causal_conv(u, sc_buf, p["conv_b_w"])
    out_b = (sb * cu) @ p["proj_b"]

    a_re = p["ssm_a_re"].astype(F32)
    a_im = p["ssm_a_im"].astype(F32)
    step = jnp.exp(p["ssm_log_dt"].astype(F32))
    mag = jnp.exp(step * a_re)
    abar_re = mag * jnp.cos(step * a_im)
    abar_im = mag * jnp.sin(step * a_im)
    den = a_re * a_re + a_im * a_im
    nr = abar_re - 1.0
    ni = abar_im
    coef_re = (nr * a_re + ni * a_im) / den
    coef_im = (ni * a_re - nr * a_im) / den
    b_re = p["ssm_b_re"].astype(F32)
    b_im = p["ssm_b_im"].astype(F32)
    bbar_re = coef_re[..., None] * b_re - coef_im[..., None] * b_im
    bbar_im = coef_re[..., None] * b_im + coef_im[..., None] * b_re
    uf = us.astype(F32)
    ug = uf.reshape(bsz, L, SSM_GROUPS, SSM_GROUP)
    bu_re = jnp.einsum('blgc,gpc->blgp', ug, bbar_re)
    bu_im = jnp.einsum('blgc,gpc->blgp', ug, bbar_im)
    h_re, h_im = complex_linear_scan(jnp.broadcast_to(abar_re, bu_re.shape),
                                     jnp.broadcast_to(abar_im, bu_im.shape),
                                     bu_re, bu_im, ssm_re.astype(F32), ssm_im.astype(F32))
    y = (jnp.einsum('blgp,gcp->blgc', h_re, p["ssm_c_re"].astype(F32))
         - jnp.einsum('blgp,gcp->blgc', h_im, p["ssm_c_im"].astype(F32)))
    y = y.reshape(bsz, L, SSM_WIDTH) + p["ssm_d"].astype(F32) * uf
    zc = jax.nn.gelu(y).astype(dt_)
    out_c = (zc * jax.nn.sigmoid(zc @ p["glu_w"] + p["glu_b"])) @ p["proj_c"]

    g = jax.nn.sigmoid(gl).reshape(bsz, L, N_BRANCH, D_MODEL)
    m = g[:, :, 0] * out_a + g[:, :, 1] * out_b + g[:, :, 2] * out_c
    out = m @ p["w_out"]
    new_states = (new_lru_conv.astype(lru_conv_buf.dtype), h_last.astype(lru_h.dtype),
                  new_sc.astype(sc_buf.dtype), h_re[:, -1].astype(ssm_re.dtype),
                  h_im[:, -1].astype(ssm_im.dtype))
    return out, new_states


def trunk_layer(x, lru_conv_buf, lru_h, sc_buf, ssm_re, ssm_im, p):
    m, st = mixer(x, lru_conv_buf, lru_h, sc_buf, ssm_re, ssm_im, p)
    x = layer_norm(ALPHA * x + m, p["ln1_g"], p["ln1_b"])
    f = jnp.square(jax.nn.relu(x @ p["mlp_up"])) @ p["mlp_down"]
    x = layer_norm(ALPHA * x + f, p["ln2_g"], p["ln2_b"])
    return x, st


def setup_inputs(seed: int = 0) -> dict:
    key = jax.random.key(seed)
    ks = jax.random.split(key, 40)
    nrm = lambda k, s, sc: jax.random.normal(k, s, F32) * sc
    rad = jax.random.uniform(ks[12], (DEPTH, LRU_WIDTH), F32, 0.9, 0.999)
    n_idx = jnp.arange(SSM_STATE, dtype=F32)
    lo, hi = math.log(0.001), math.log(0.1)
    return {
        "x_prompt": nrm(ks[0], (BATCH, SEQ, D_MODEL), 1.0),
        "x_sample": nrm(ks[1], (DEC_BATCH, DEC_SEQ, D_MODEL), 1.0),
        "state_lru_conv": nrm(ks[2], (DEPTH, DEC_BATCH, LRU_CONV - 1, LRU_WIDTH), 1.0),
        "state_lru_h": nrm(ks[3], (DEPTH, DEC_BATCH, LRU_WIDTH), 0.5),
        "state_sconv": nrm(ks[4], (DEPTH, DEC_BATCH, SC_CONV - 1, SC_WIDTH), 0.5),
        "state_ssm_re": nrm(ks[5], (DEPTH, DEC_BATCH, SSM_GROUPS, SSM_STATE), 0.3),
        "state_ssm_im": nrm(ks[6], (DEPTH, DEC_BATCH, SSM_GROUPS, SSM_STATE), 0.3),
        "w_in": nrm(ks[7], (DEPTH, D_MODEL, N_IN), D_MODEL ** -0.5),
        "conv_a_w": nrm(ks[8], (DEPTH, LRU_CONV, LRU_WIDTH), LRU_CONV ** -0.5),
        "conv_a_b": nrm(ks[9], (DEPTH, LRU_WIDTH), 0.01),
        "gate_x_w": nrm(ks[10], (DEPTH, LRU_HEADS, LRU_HEAD_DIM, LRU_HEAD_DIM), LRU_HEAD_DIM ** -0.5),
        "gate_x_b": nrm(ks[11], (DEPTH, LRU_WIDTH), 0.01),
        "gate_a_w": nrm(ks[13], (DEPTH, LRU_HEADS, LRU_HEAD_DIM, LRU_HEAD_DIM), LRU_HEAD_DIM ** -0.5),
        "gate_a_b": nrm(ks[14], (DEPTH, LRU_WIDTH), 0.01),
        "lru_lambda": jnp.log(rad) - jnp.log1p(-rad),
        "conv_b_w": nrm(ks[15], (DEPTH, SC_CONV, SC_WIDTH), SC_CONV ** -0.5),
        "ssm_a_re": -0.5 + nrm(ks[16], (DEPTH, SSM_GROUPS, SSM_STATE), 0.01),
        "ssm_a_im": math.pi * n_idx + nrm(ks[17], (DEPTH, SSM_GROUPS, SSM_STATE), 0.01),
        "ssm_log_dt": lo + (hi - lo) * jax.random.uniform(ks[18], (DEPTH, SSM_GROUPS, SSM_STATE), F32),
        "ssm_b_re": nrm(ks[19], (DEPTH, SSM_GROUPS, SSM_STATE, SSM_GROUP), (2 * SSM_GROUP) ** -0.5),
        "ssm_b_im": nrm(ks[20], (DEPTH, SSM_GROUPS, SSM_STATE, SSM_GROUP), (2 * SSM_GROUP) ** -0.5),
        "ssm_c_re": nrm(ks[21], (DEPTH, SSM_GROUPS, SSM_GROUP, SSM_STATE), (2 * SSM_STATE) ** -0.5),
        "ssm_c_im": nrm(ks[22], (DEPTH, SSM_GROUPS, SSM_GROUP, SSM_STATE), (2 * SSM_STATE) ** -0.5),
        "ssm_d": nrm(ks[23], (DEPTH, SSM_WIDTH), 1.0),
        "glu_w": nrm(ks[24], (DEPTH, SSM_WIDTH, SSM_WIDTH), SSM_WIDTH ** -0.5),
        "glu_b": nrm(ks[25], (DEPTH, SSM_WIDTH), 0.01),
        "proj_a": nrm(ks[26], (DEPTH, LRU_WIDTH, D_MODEL), LRU_WIDTH ** -0.5),
        "proj_b": nrm(ks[27], (DEPTH, SC_WIDTH, D_MODEL), SC_WIDTH ** -0.5),
        "proj_c": nrm(ks[28], (DEPTH, SSM_WIDTH, D_MODEL), SSM_WIDTH ** -0.5),
        "w_out": nrm(ks[29], (DEPTH, D_MODEL, D_MODEL), BETA * D_MODEL ** -0.5),
        "ln1_g": 1.0 + nrm(ks[30], (DEPTH, D_MODEL), 0.01),
        "ln1_b": nrm(ks[31], (DEPTH, D_MODEL), 0.01),
        "mlp_up": nrm(ks[32], (DEPTH, D_MODEL, D_FF), D_MODEL ** -0.5),
        "mlp_down": nrm(ks[33], (DEPTH, D_FF, D_MODEL), BETA * D_FF ** -0.5),
        "ln2_g": 1.0 + nrm(ks[34], (DEPTH, D_MODEL), 0.01),
        "ln2_b": nrm(ks[35], (DEPTH, D_MODEL), 0.01),
    }


def reference(x_prompt, x_sample, state_lru_conv, state_lru_h, state_sconv, state_ssm_re, state_ssm_im,
              w_in, conv_a_w, conv_a_b, gate_x_w, gate_x_b, gate_a_w, gate_a_b, lru_lambda, conv_b_w,
              ssm_a_re, ssm_a_im, ssm_log_dt, ssm_b_re, ssm_b_im, ssm_c_re, ssm_c_im, ssm_d, glu_w, glu_b,
              proj_a, proj_b, proj_c, w_out, ln1_g, ln1_b, mlp_up, mlp_down, ln2_g, ln2_b):
    dt_ = x_prompt.dtype
    z_lru_conv = jnp.zeros((BATCH, LRU_CONV - 1, LRU_WIDTH), dt_)
    z_lru_h = jnp.zeros((BATCH, LRU_WIDTH), dt_)
    z_sconv = jnp.zeros((BATCH, SC_CONV - 1, SC_WIDTH), dt_)
    z_ssm = jnp.zeros((BATCH, SSM_GROUPS, SSM_STATE), dt_)

    yp, ys = x_prompt, x_sample
    sp = [[] for _ in range(5)]
    ss = [[] for _ in range(5)]
    for i in range(DEPTH):
        p = {
            "w_in": w_in[i], "conv_a_w": conv_a_w[i], "conv_a_b": conv_a_b[i],
            "gate_x_w": gate_x_w[i], "gate_x_b": gate_x_b[i], "gate_a_w": gate_a_w[i],
            "gate_a_b": gate_a_b[i], "lru_lambda": lru_lambda[i], "conv_b_w": conv_b_w[i],
            "ssm_a_re": ssm_a_re[i], "ssm_a_im": ssm_a_im[i], "ssm_log_dt": ssm_log_dt[i],
            "ssm_b_re": ssm_b_re[i], "ssm_b_im": ssm_b_im[i], "ssm_c_re": ssm_c_re[i],
            "ssm_c_im": ssm_c_im[i], "ssm_d": ssm_d[i], "glu_w": glu_w[i], "glu_b": glu_b[i],
            "proj_a": proj_a[i], "proj_b": proj_b[i], "proj_c": proj_c[i], "w_out": w_out[i],
            "ln1_g": ln1_g[i], "ln1_b": ln1_b[i], "mlp_up": mlp_up[i], "mlp_down": mlp_down[i],
            "ln2_g": ln2_g[i], "ln2_b": ln2_b[i],
        }
        yp, stp = trunk_layer(yp, z_lru_conv, z_lru_h, z_sconv, z_ssm, z_ssm, p)
        ys, sts = trunk_layer(ys, state_lru_conv[i], state_lru_h[i], state_sconv[i],
                              state_ssm_re[i], state_ssm_im[i], p)
        for j in range(5):
            sp[j].append(stp[j])
            ss[j].append(sts[j])
    new_lru_conv_p = jnp.stack(sp[0], 0)
    new_lru_h_p = jnp.stack(sp[1], 0)
    new_sconv_p = jnp.stack(sp[2], 0)
    new_ssm_re_p = jnp.stack(sp[3], 0)
    new_ssm_im_p = jnp.stack(sp[4], 0)
    new_lru_conv_s = jnp.stack(ss[0], 0)
    new_lru_h_s = jnp.stack(ss[1], 0)
    new_sconv_s = jnp.stack(ss[2], 0)
    new_ssm_re_s = jnp.stack(ss[3], 0)
    new_ssm_im_s = jnp.stack(ss[4], 0)
    return (yp, ys, new_lru_conv_p, new_lru_h_p, new_sconv_p, new_ssm_re_p, new_ssm_im_p,
            new_lru_conv_s, new_lru_h_s, new_sconv_s, new_ssm_re_s, new_ssm_im_s)
```

```python
import math
from contextlib import ExitStack
import numpy as np
import concourse.bass as bass
import concourse.mybir as mybir
from concourse.bass_utils import run_bass_kernel_spmd

F32 = mybir.dt.float32
BF16 = mybir.dt.bfloat16
I32 = mybir.dt.int32
AF = mybir.ActivationFunctionType
ALU = mybir.AluOpType

NCORES = 8
D = 1024
DEPTH = 2
SEQ = 2048
NSS = 16
DL = 4
G = 32
TB = 128
ALPHA = (2 * DEPTH) ** 0.25
MAGIC = 12582912.0
TWO_PI = 2.0 * math.pi
S5ADD = "pool"
POOLX = "pool"
STRICT_WAR = True

ENG_ATTR = {"pe": "tensor", "act": "scalar", "dve": "vector", "pool": "gpsimd", "sp": "sync"}


class Op:
    __slots__ = ("eng", "fn", "deps", "odeps", "signal", "sigval", "dma", "dsem", "dval", "dur", "nbytes", "seq",
                 "succ", "npred", "t_end", "output")

    def __init__(self, eng, fn, dma):
        self.eng = eng
        self.fn = fn
        self.deps = []
        self.odeps = []
        self.signal = False
        self.sigval = 0
        self.dma = dma
        self.dsem = None
        self.dval = 0
        self.dur = 0.3
        self.nbytes = 0
        self.output = False


class Sched:
    def __init__(self, nc):
        self.nc = nc
        self.all = []
        self.last_w = {}
        self.readers = {}
        self.ndma = {"sp": 8, "pool": 6, "act": 2}
        self.reorder = True
        self.alias = {}

    def add(self, eng, fn, reads=(), writes=(), dma=False, output=False, dur=None, nbytes=0):
        op = Op(eng, fn, dma)
        op.seq = len(self.all)
        op.output = output
        op.nbytes = nbytes
        if dur is not None:
            op.dur = dur
        deps = {}
        al = self.alias
        reads = list(reads) + [r for k in reads for r in al.get(k, ())]
        writes = list(writes) + [r for k in writes for r in al.get(k, ())]
        writes = list(writes) + [k for k in reads if k[0] == "ps" and k not in writes]
        for k in reads:
            w = self.last_w.get(k)
            if w is not None:
                deps[id(w)] = (w, True)
        for k in writes:
            w = self.last_w.get(k)
            if w is not None:
                deps[id(w)] = (w, True)
            for r in self.readers.get(k, ()):
                if id(r) not in deps:
                    deps[id(r)] = (r, False)
        for d, strong in deps.values():
            if d is op:
                continue
            op.odeps.append(d)
            if (not d.dma) and d.eng == eng:
                if eng == "pe" or not (strong or STRICT_WAR):
                    continue
            op.deps.append(d)
        for k in reads:
            self.readers.setdefault(k, []).append(op)
        for k in writes:
            self.last_w[k] = op
            self.readers[k] = []
        self.all.append(op)
        return op

    def schedule(self):
        import heapq
        ops = self.all
        for op in ops:
            op.succ = []
            op.npred = 0
        for op in ops:
            seen = set()
            for d in op.odeps:
                if id(d) in seen:
                    continue
                seen.add(id(d))
                d.succ.append(op)
                op.npred += 1
        order = {e: [] for e in ENG_ATTR}
        if not self.reorder:
            for op in ops:
                order[op.eng].append(op)
            return order
        free = {e: 0.0 for e in ENG_ATTR}
        ready = {e: [] for e in ENG_ATTR}
        rtime = {}
        for op in ops:
            if op.npred == 0:
                heapq.heappush(ready[op.eng], (op.seq, op))
                rtime[id(op)] = 0.0
        dma_pipe = [0.0]
        nleft = len(ops)
        use_cp = getattr(self, "use_cp", False)
        if use_cp:
            for op in reversed(ops):
                d = (op.nbytes / 160e3 + 2.0) if op.dma else op.dur
                op.sigval = d + max([s_.sigval for s_ in op.succ], default=0.0)
        LOOK = getattr(self, 'look', 24)
        while nleft:
            best = None
            for e in ENG_ATTR:
                h = ready[e]
                if not h:
                    continue
                cand = heapq.nsmallest(LOOK, h)
                for seq, op in cand:
                    est = max(free[e], rtime[id(op)])
                    key = (est, -op.sigval, seq) if use_cp else (est, seq)
                    if best is None or key < best[0]:
                        best = (key, e, op)
            est, e, op = best[0][0], best[1], best[2]
            ready[e].remove((op.seq, op))
            heapq.heapify(ready[e])
            if op.dma:
                issue = 0.9 if e == "pool" else 0.15
                t0 = est + issue
                xfer = op.nbytes / getattr(self, 'dma_bw', 300e3)
                st = max(t0, dma_pipe[0])
                dma_pipe[0] = st + xfer
                op.t_end = st + xfer + 2.0
                free[e] = t0
            else:
                op.t_end = est + op.dur
                free[e] = op.t_end
            order[e].append(op)
            nleft -= 1
            for s_ in op.succ:
                s_.npred -= 1
                lat = 0.05 if (s_.eng == e and not op.dma) else 0.2
                rt = max(rtime.get(id(s_), 0.0), op.t_end + lat)
                rtime[id(s_)] = rt
                if s_.npred == 0:
                    heapq.heappush(ready[s_.eng], (s_.seq, s_))
        self.makespan = max(free.values())
        return order

    def emit(self, es):
        nc = self.nc
        order = self.schedule()
        for op in self.all:
            op.sigval = 0
        out_dmas = []
        for q, n in self.ndma.items():
            c = 0
            last = {}
            for op in order[q]:
                if not op.dma:
                    continue
                slot = c % n
                op.dsem = (q, slot)
                op.dval = 16 * (c // n + 1)
                prev = last.get(slot)
                if prev is not None:
                    op.deps.append(prev)
                last[slot] = op
                c += 1
                if op.output:
                    out_dmas.append(op)
        for e in ENG_ATTR:
            for op in order[e]:
                for d in op.deps:
                    if not d.dma:
                        d.signal = True
        fin = Op("sp", None, False)
        fin.deps = list(out_dmas)
        order["sp"].append(fin)
        sems = {e: es.enter_context(nc.semaphore("sem_" + e)) for e in ENG_ATTR}
        dsems = {}
        for q, n in self.ndma.items():
            for s in range(n):
                dsems[(q, s)] = es.enter_context(nc.semaphore("dsem_%s_%d" % (q, s)))
        for e in ENG_ATTR:
            c = 0
            for op in order[e]:
                if op.signal:
                    c += 1
                    op.sigval = c
        block = es.enter_context(nc.Block())

        def run(eng_handle, e):
            waited = {}
            for op in order[e]:
                need = {}
                for d in op.deps:
                    if d.dma:
                        key = ("d",) + d.dsem
                        sem = dsems[d.dsem]
                        val = d.dval
                    else:
                        key = ("e", d.eng)
                        sem = sems[d.eng]
                        val = d.sigval
                    if val > need.get(key, (None, 0))[1]:
                        need[key] = (sem, val)
                for key, (sem, val) in need.items():
                    if waited.get(key, 0) >= val:
                        continue
                    waited[key] = val
                    eng_handle.wait_ge(sem, val)
                if op.fn is None:
                    continue
                ins = op.fn(eng_handle)
                if op.dma:
                    ins.then_inc(dsems[op.dsem], 16)
                elif op.signal:
                    ins.then_inc(sems[e], 1)

        for e, attr in ENG_ATTR.items():
            getattr(block, attr)(lambda h, e=e: run(h, e))


class _Stop(Exception):
    pass


def build_program(dbg=None):
    dbg = dbg or {}

    def chk(name):
        if dbg.get('stop') == name:
            raise _Stop()
    nc = bass.Bass("TRN2", target_bir_lowering=False)
    es = ExitStack()
    S = Sched(nc)
    S.reorder = not dbg.get('noreorder', False)
    if 'look' in dbg:
        S.look = dbg['look']
    S.use_cp = dbg.get('use_cp', True)
    if 'dma_bw' in dbg:
        S.dma_bw = dbg['dma_bw']

    def din(name, shape):
        return nc.dram_tensor(name, list(shape), F32, kind="ExternalInput").ap()

    def dout(name, shape):
        return nc.dram_tensor(name, list(shape), F32, kind="ExternalOutput").ap()

    xp = din("xp", [SEQ, D])
    xs = din("xs", [NSS * DL, D])
    st_conv = din("st_conv", [DEPTH, NSS, 3, 512])
    st_h = din("st_h", [DEPTH, NSS, 512])
    st_sc = din("st_sc", [DEPTH, NSS, 2, 512])
    st_re = din("st_re", [DEPTH, NSS, 2048])
    st_im = din("st_im", [DEPTH, NSS, 2048])
    w_in = din("w_in", [DEPTH, D, 6144])
    conv_a_w = din("conv_a_w", [DEPTH, 4, 512])
    conv_a_b = din("conv_a_b", [DEPTH, 512])
    gate_x_w = din("gate_x_w", [DEPTH, 8, 64, 64])
    gate_x_b = din("gate_x_b", [DEPTH, 512])
    gate_a_w = din("gate_a_w", [DEPTH, 8, 64, 64])
    gate_a_b = din("gate_a_b", [DEPTH, 512])
    lru_lambda = din("lru_lambda", [DEPTH, 512])
    conv_b_w = din("conv_b_w", [DEPTH, 3, 512])
    ssm_a_re = din("ssm_a_re", [DEPTH, 2048])
    ssm_a_im = din("ssm_a_im", [DEPTH, 2048])
    ssm_log_dt = din("ssm_log_dt", [DEPTH, 2048])
    ssm_b_re = din("ssm_b_re", [DEPTH, 2048 * 16])
    ssm_b_im = din("ssm_b_im", [DEPTH, 2048 * 16])
    ssm_c_re = din("ssm_c_re", [DEPTH, 512 * 64])
    ssm_c_im = din("ssm_c_im", [DEPTH, 512 * 64])
    ssm_d = din("ssm_d", [DEPTH, 512])
    glu_w = din("glu_w", [DEPTH, 512, 512])
    glu_b = din("glu_b", [DEPTH, 512])
    proj_a = din("proj_a", [DEPTH, 512, D])
    proj_b = din("proj_b", [DEPTH, 512, D])
    proj_c = din("proj_c", [DEPTH, 512, D])
    w_out = din("w_out", [DEPTH, D, D])
    ln1_g = din("ln1_g", [DEPTH, D])
    ln1_b = din("ln1_b", [DEPTH, D])
    mlp_up = din("mlp_up", [DEPTH, D, 4096])
    mlp_down = din("mlp_down", [DEPTH, 4096, D])
    ln2_g = din("ln2_g", [DEPTH, D])
    ln2_b = din("ln2_b", [DEPTH, D])

    yp = dout("yp", [SEQ, D])
    ys = dout("ys", [NSS * DL, D])
    o_conv_p = dout("o_conv_p", [DEPTH, 3, 512])
    o_h_p = dout("o_h_p", [DEPTH, 512])
    o_sc_p = dout("o_sc_p", [DEPTH, 2, 512])
    o_re_p = dout("o_re_p", [DEPTH, 2048])
    o_im_p = dout("o_im_p", [DEPTH, 2048])
    o_conv_s = dout("o_conv_s", [DEPTH, NSS, 3, 512])
    o_h_s = dout("o_h_s", [DEPTH, NSS, 512])
    o_sc_s = dout("o_sc_s", [DEPTH, NSS, 2, 512])
    o_re_s = dout("o_re_s", [DEPTH, NSS, 2048])
    o_im_s = dout("o_im_s", [DEPTH, NSS, 2048])

    es.enter_context(nc.allow_non_contiguous_dma(reason="small transposing loads/stores of per-feature vectors and states"))

    def sb(name, shape, dt=F32):
        return es.enter_context(nc.sbuf_tensor(name, list(shape), dt))

    NR = 7
    ring = [sb("ring%d" % i, [128, 4096], BF16) for i in range(NR)]
    ring_i = [0]
    x_f = sb("x_f", [128, 8, 512])
    x_b = sb("x_b", [128, 8, 512], BF16)
    VEC = {}
    nv = [0]

    def vslot(name, n):
        VEC[name] = nv[0]
        nv[0] += n

    for k in range(4):
        vslot("caw%d" % k, 4)
    vslot("cab", 4)
    vslot("gxb", 4)
    vslot("gab", 4)
    vslot("lam", 4)
    for k in range(3):
        vslot("cbw%d" % k, 4)
    vslot("ssd", 4)
    vslot("glb", 4)
    vslot("l1g", 8)
    vslot("l1b", 8)
    vslot("l2g", 8)
    vslot("l2b", 8)
    vslot("lc", 4)
    vslot("tmp", 4)
    NV = nv[0]
    vecs = [sb("vecs%d" % l, [128, NV]) for l in range(DEPTH)]
    gxw = [sb("gxw%d" % l, [128, 4, 128], BF16) for l in range(DEPTH)]
    gaw = [sb("gaw%d" % l, [128, 4, 128], BF16) for l in range(DEPTH)]
    ident = sb("ident", [128, 128])
    ones_b = sb("ones_b", [128, 128], BF16)
    cst = sb("cst", [128, 4])
    s5c = [sb("s5c%d" % l, [128, 12, 16]) for l in range(DEPTH)]
    C_R, C_AR, C_AI, C_NAI, C_CR, C_CI, C_NCI = 0, 1, 2, 3, 4, 5, 6
    NTB = 4
    tabbuf = [sb("tabbuf%d" % i, [128, 512]) for i in range(NTB)]
    tabs = nc.dram_tensor("tabs_scratch", [DEPTH, 16, 2, 128, 512], F32).ap()
    NBC = 4
    bcbuf = [sb("bcbuf%d" % i, [128, 5, 128], BF16) for i in range(NBC)]
    bc_scr = nc.dram_tensor("bc_scratch", [DEPTH, 16, 128, 4, 128], BF16).ap()
    us_f = sb("us_f", [128, 2048])
    us_b = sb("us_b", [128, 2048], BF16)
    convA_st = [sb("convA_st%d" % l, [128, 4, 3]) for l in range(DEPTH)]
    convB_st = [sb("convB_st%d" % l, [128, 4, 2]) for l in range(DEPTH)]
    lru_h = [sb("lru_h%d" % l, [128, 4]) for l in range(DEPTH)]
    ssm_h = [sb("ssm_h%d" % l, [128, 16, 2]) for l in range(DEPTH)]
    s_h0 = sb("s_h0", [128, 4, NSS])
    s_re0 = sb("s_re0", [128, 16, NSS])
    s_im0 = sb("s_im0", [128, 16, NSS])
    s_sc0 = sb("s_sc0", [128, 4, NSS, 2])
    sst = sb("sst", [128, 8, 128])
    stg_o = sb("stg_o", [128, 8, 128])
    out_rows = sst
    arena = sb("arena", [128, 6144])
    v_b = sb("v_b", [128, 4, 512], BF16)
    m_f = sb("m_f", [128, 8, 512])
    NT = 8
    tf = [sb("tf%d" % i, [128, 512]) for i in range(NT)]
    NTBF = 5
    tb_ = [sb("tb%d" % i, [128, 512], BF16) for i in range(NTBF)]
    tf_i = [0]
    tb_i = [0]
    small = sb("small", [128, 64])
    psum = [es.enter_context(nc.psum_tensor("ps%d" % i, [128, 512], F32)) for i in range(8)]
    ps_i = [0]
    ps_nrot = [6]

    def next_ps():
        p = psum[ps_i[0] % ps_nrot[0]]
        ps_i[0] += 1
        return p

    class _TV:
        def __init__(self, ap, name):
            self.ap = ap
            self.name = name

        def __getitem__(self, key):
            return self.ap[key]

    tf_pool = list(tf)
    tf_n = [NT]

    def tmpf():
        t = tf_pool[tf_i[0] % tf_n[0]]
        tf_i[0] += 1
        return t

    def tmpb():
        t = tb_[tb_i[0] % NTBF]
        tb_i[0] += 1
        return t

    upad = arena[:, 0:2080]
    xc_f = arena[:, 2080:4128]
    xc_b = arena[:, 4128:5152].bitcast(BF16)
    r_f = arena[:, 0:4096]
    r_b = arena[:, 4096:6144].bitcast(BF16)
    m_b = arena[:, 4096:6144].bitcast(BF16).rearrange("p (f n) -> p f n", f=8)
    xin = arena[:, 0:4096]
    zc_t = sb("zc_t", [128, 4, 512], BF16)
    hid_v = m_f[:, :, :].rearrange("p f n -> p (f n)").bitcast(BF16).rearrange("p (j n) -> p j n", j=16)
    yout = arena[:, 0:4096]

    NS_ = NSS * DL
    x_f_s = sb("x_f_s", [128, 8, NS_])
    x_b_s = sb("x_b_s", [128, 8, NS_], BF16)
    arena_s = sb("arena_s", [128, 1024])
    v_b_s = sb("v_b_s", [128, 4, NS_], BF16)
    m_f_s = sb("m_f_s", [128, 8, NS_])
    m_b_s = sb("m_b_s", [128, 8, NS_], BF16)
    zc_t_s = sb("zc_t_s", [128, 4, NS_], BF16)
    us_f_s = sb("us_f_s", [128, 4 * NS_])
    us_b_s = sb("us_b_s", [128, 4 * NS_], BF16)

    class Ctx:
        pass

    CP = Ctx()
    CP.tag = None
    CP.t = dict(x_f=x_f, x_b=x_b, upad=upad, xc_f=xc_f, xc_b=xc_b, r_f=r_f, r_b=r_b, xin=xin, yout=yout, v_b=v_b, m_f=m_f, m_b=m_b,
                zc_t=zc_t, hid_v=hid_v, us_f=us_f, us_b=us_b)
    CS = Ctx()
    CS.tag = "s"
    CS.t = dict(x_f=x_f_s, x_b=x_b_s, upad=arena_s[:, 0:448], xc_f=arena_s[:, 448:704], xc_b=arena_s[:, 704:832].bitcast(BF16),
                r_f=arena_s[:, 0:512], r_b=arena_s[:, 512:768].bitcast(BF16), xin=arena_s[:, 0:1024], yout=arena_s[:, 0:1024],
                v_b=v_b_s, m_f=m_f_s, m_b=m_b_s, zc_t=zc_t_s,
                hid_v=m_f_s[:, :, :].rearrange("p f n -> p (f n)").bitcast(BF16).rearrange("p (j n) -> p j n", j=16),
                us_f=us_f_s, us_b=us_b_s)
    tf_extra = [(x_f_s[:, :, :].rearrange("p f n -> p (f n)"), "tfx0", [("s", "x_f", f_) for f_ in range(8)]),
                (m_f_s[:, :, :].rearrange("p f n -> p (f n)"), "tfx1", [("s", "m_f", f_) for f_ in range(8)] + [("s", "hid")]),
                (arena_s[:, 0:512], "tfx2", [("ars", 0)]),
                (arena_s[:, 512:1024], "tfx3", [("ars", 1)])]
    cur_tag = [None]
    CTXKEYS = {"x_b", "x_f", "upad", "xc_f", "xc_b", "r_f", "r_b", "v_b", "m_f", "m_b", "zc", "hid", "us_f", "us_b", "xin", "yout"}

    def activate(C):
        nonlocal x_f, x_b, upad, xc_f, xc_b, r_f, r_b, xin, yout, v_b, m_f, m_b, zc_t, hid_v, us_f, us_b
        t = C.t
        x_f, x_b, upad, xc_f, xc_b, r_f, r_b = t["x_f"], t["x_b"], t["upad"], t["xc_f"], t["xc_b"], t["r_f"], t["r_b"]
        xin, yout, v_b, m_f, m_b, zc_t, hid_v, us_f, us_b = t["xin"], t["yout"], t["v_b"], t["m_f"], t["m_b"], t["zc_t"], t["hid_v"], t["us_f"], t["us_b"]
        cur_tag[0] = C.tag

    def K(*a):
        if cur_tag[0] is not None and a[0] in CTXKEYS:
            return (cur_tag[0],) + a
        return a

    def blocks(tag, lo, hi, bs=512):
        return [(tag, b) for b in range(lo // bs, (hi - 1) // bs + 1)]

    AL = S.alias
    AL[K("upad")] = blocks("ar", 0, 2080)
    AL[K("xin")] = blocks("ar", 0, 4096)
    AL[K("yout")] = blocks("ar", 0, 4096)
    AL[K("r_f")] = blocks("ar", 0, 4096)
    AL[K("r_b")] = blocks("ar", 4096, 6144)
    AL[K("BTs")] = blocks("ar", 0, 2048)
    AL[K("CTs")] = blocks("ar", 2048, 4096)
    AL[K("masks")] = blocks("ar", 4096, 5120)
    AL[K("cdup")] = blocks("ar", 5120, 6144)
    for a_ in range(4):
        AL[K("xc_f", a_)] = blocks("ar", 2080 + 512 * a_, 2080 + 512 * (a_ + 1))
        AL[K("xc_b", a_)] = blocks("ar", 4128 + 256 * a_, 4128 + 256 * (a_ + 1))
    for f_ in range(8):
        AL[K("m_f", f_)] = [("mf", f_)]
        AL[K("m_b", f_)] = blocks("ar", 4096 + 256 * f_, 4096 + 256 * (f_ + 1))
    AL[K("hid")] = [("mf", f_) for f_ in range(8)]
    AL[K("braw")] = [("mf", 0)]
    AL[K("bbar")] = [("mf", 1)]
    for j_ in range(4):
        AL[K("bpad", j_)] = [("mf", 2)]
    AL[K("s5t", 0)] = [("mf", 3)]
    AL[K("s5t", 1)] = [("mf", 3)]
    AL[K("gstage")] = [("mf", 4)]
    AL[K("io_i")] = [("mf", 5)]
    AL[K("tau_i")] = [("mf", 5), ("mf", 6)]
    AL[K("vstage")] = [("mf", 6)]
    AL[K("sstage")] = [("mf", 7)]
    AL[("s", "upad")] = blocks("ars", 0, 448)
    AL[("s", "xin")] = blocks("ars", 0, 1024)
    AL[("s", "yout")] = blocks("ars", 0, 1024)
    AL[("s", "r_f")] = blocks("ars", 0, 512)
    AL[("s", "r_b")] = blocks("ars", 512, 768)
    for a_ in range(4):
        AL[("s", "xc_f", a_)] = blocks("ars", 448 + 64 * a_, 448 + 64 * (a_ + 1))
        AL[("s", "xc_b", a_)] = blocks("ars", 704 + 32 * a_, 704 + 32 * (a_ + 1))
    for f_ in range(8):
        AL[("s", "m_f", f_)] = [("mfs", 0)]
    AL[("s", "hid")] = [("mfs", 0)]

    def act(out, in_, func, reads, writes, bias=None, scale=None):
        kw = {}
        if bias is not None:
            kw["bias"] = bias
        if scale is not None:
            kw["scale"] = scale
        return S.add("act", lambda e: e.activation(out=out, in_=in_, func=func, **kw), list(reads) + [("cst",)], writes, dur=0.22 + out.free_size() / 1400.0)

    def edur(eng, n, c):
        if eng == "pool":
            return 0.25 + n * 2.0 / 1000.0
        return 0.12 + n * c / 960.0

    def tt(out, in0, in1, op, reads, writes, eng="dve"):
        c = 1.0 if any(k[0] == "ps" for k in reads) else 2.0
        return S.add(eng, lambda e: e.tensor_tensor(out=out, in0=in0, in1=in1, op=op), reads, writes, dur=edur(eng, out.free_size(), c))

    def ts(out, in0, s1, s2, op0, op1, reads, writes, eng="dve"):
        if op1 is None:
            return S.add(eng, lambda e: e.tensor_scalar(out=out, in0=in0, scalar1=s1, scalar2=None, op0=op0), reads, writes, dur=edur(eng, out.free_size(), 1.0))
        return S.add(eng, lambda e: e.tensor_scalar(out=out, in0=in0, scalar1=s1, scalar2=s2, op0=op0, op1=op1), reads, writes, dur=edur(eng, out.free_size(), 1.0))

    def stt(out, in0, scalar, in1, op0, op1, reads, writes, eng="dve"):
        return S.add(eng, lambda e: e.scalar_tensor_tensor(out=out, in0=in0, scalar=scalar, in1=in1, op0=op0, op1=op1), reads, writes, dur=edur(eng, out.free_size(), 1.4))

    def cp(out, in_, reads, writes, eng="dve"):
        if eng == "act":
            return S.add(eng, lambda e: e.activation(out=out, in_=in_, func=AF.Copy), reads, writes, dur=0.22 + out.free_size() / 1400.0)
        return S.add(eng, lambda e: e.tensor_copy(out=out, in_=in_), reads, writes, dur=edur(eng, out.free_size(), 1.0))

    def mset(ap, val, writes, eng="dve"):
        return S.add(eng, lambda e: e.memset(ap, val), (), writes)

    def dma(q, out, in_, reads, writes, output=False):
        return S.add(q, lambda e: e.dma_start(out=out, in_=in_), reads, writes, dma=True, output=output, nbytes=4 * out.size())

    def mm(ps, n, pairs, reads):
        last = len(pairs) - 1
        for i, (l_, r_) in enumerate(pairs):
            S.add("pe", lambda e, l_=l_, r_=r_, i=i: e.matmul(ps[:, 0:n], lhsT=l_, rhs=r_, start=(i == 0), stop=(i == last)),
                  reads, [K("ps", ps.name)], dur=0.015 + n / 2150.0)

    def load_w(src3, kt, ncol):
        i = ring_i[0] % NR
        ring_i[0] += 1
        slot = ring[i]
        view = slot[:, 0:kt * ncol].rearrange("p (k c) -> p k c", k=kt)
        S.add("pool", lambda e: e.dma_start(out=view, in_=src3), (), [K("ring", i)], dma=True, nbytes=4 * 128 * kt * ncol)
        return view, K("ring", i)

    def w_rows(mat2d, r0, nrow, c0, ncol):
        return mat2d[r0:r0 + nrow, c0:c0 + ncol].rearrange("(k p) c -> p k c", p=128)

    mflat0 = m_f[:, :, :].rearrange("p f n -> p (f n)")
    io_i = mflat0[:, 2560:2688].bitcast(I32)
    S.add("pool", lambda e: e.iota(io_i[:], pattern=[[1, 128]], base=0, channel_multiplier=-1), (), [K("io_i")])
    S.add("dve", lambda e: e.tensor_single_scalar(out=ident[:], in_=io_i[:], scalar=0, op=ALU.is_equal), [K("io_i")], [K("ident")])
    tau_i = mflat0[:, 2944:3456].bitcast(I32)
    tau_f = sb("tau_f", [128, 512])
    S.add("pool", lambda e: e.iota(tau_i[:], pattern=[[1, 512]], base=1, channel_multiplier=0), (), [K("tau_i")])
    cp(tau_f[:], tau_i[:], [K("tau_i")], [K("tau_f")])
    mset(ones_b[:], 1.0, [K("ones_b")])
    mset(cst[:, 0:1], 1e-5, [K("cst")])
    mset(cst[:, 1:2], 1.0, [K("cst")])
    mset(cst[:, 2:3], 0.0, [K("cst")])
    EPS = cst[:, 0:1]
    ONE = cst[:, 1:2]
    for l in range(DEPTH):
        mset(convA_st[l][:], 0.0, [K("convA_st", l)])
        mset(convB_st[l][:], 0.0, [K("convB_st", l)])
        mset(lru_h[l][:], 0.0, [K("lru_h", l)])
        mset(ssm_h[l][:], 0.0, [K("ssm_h", l, i) for i in range(16)])

    def V(l, name, a):
        c = VEC[name] + a
        return vecs[l][:, c:c + 1]

    masks = arena[:, 4096:5120].rearrange("p (m c) -> p m c", m=8)
    mset(masks[:], 0.0, [K("masks")])
    for jj in range(4):
        for sg, val in ((0, 1.0), (1, -1.0)):
            j0 = 2 * jj
            mset(masks[0:64, 2 * jj + sg, 16 * j0:16 * j0 + 16], val, [K("masks")])
            mset(masks[64:128, 2 * jj + sg, 16 * (j0 + 1):16 * (j0 + 1) + 16], val, [K("masks")])

    mflat = m_f[:, :, :].rearrange("p f n -> p (f n)")
    BTs = arena[:, 0:2048].bitcast(BF16).rearrange("p (i r c) -> p i r c", i=16, r=2)
    CTs = arena[:, 2048:4096].bitcast(BF16).rearrange("p (i r c) -> p i r c", i=16, r=2)
    braw = mflat[:, 0:512].rearrange("p (r i c) -> p r i c", r=2, i=16)
    bbar = mflat[:, 512:1024].rearrange("p (r i c) -> p r i c", r=2, i=16)
    bpads = [mflat[:, 1024 + 128 * j:1152 + 128 * j] for j in range(4)]
    s5t = mflat[:, 1536:1728].rearrange("p (k i) -> p k i", k=12)
    gstage = mflat[:, 2048:2560].rearrange("p (a c) -> p a c", a=4)
    cdup = arena[:, 5120:6144].rearrange("p (r a c) -> p r a c", r=2, a=4)
    ang = arena[:, 0:2048].rearrange("p (i t) -> p i t", i=16)
    frc = arena[:, 2048:4096].rearrange("p (i t) -> p i t", i=16)
    for j in range(4):
        mset(bpads[j], 0.0, [K("bpad", j)])

    def sincos(ang_ap, sin_out, cos_out, tmp1, tmp2, key_in, key_s, key_c, k1, k2):
        ts(tmp1, ang_ap, 1.0 / TWO_PI, MAGIC, ALU.mult, ALU.add, [key_in], [k1])
        ts(tmp1, tmp1, -MAGIC, None, ALU.add, None, [k1], [k1])
        stt(tmp1, ang_ap, 1.0 / TWO_PI, tmp1, ALU.mult, ALU.subtract, [key_in, k1], [k1])
        act(sin_out, tmp1, AF.Sin, [k1], [key_s], scale=TWO_PI)
        act(tmp2, tmp1, AF.Sin, [k1, key_in], [k2], scale=math.pi)
        tt(tmp2, tmp2, tmp2, ALU.mult, [k2], [k2])
        ts(cos_out, tmp2, -2.0, 1.0, ALU.mult, ALU.add, [k2], [key_c])

    def prep_tables(l):
        kc = K("s5c", l)
        for i in range(16):
            angp, frcp, sinp, cosp = tmpf(), tmpf(), tmpf(), tmpf()
            ts(angp[:, :], tau_f[:, :], s5c[l][:, 7, i:i + 1], None, ALU.mult, None, [K("tau_f"), kc], [K(angp.name)])
            sincos(angp[:, :], sinp[:, :], cosp[:, :], frcp[:, :], angp[:, :], K(angp.name), K(sinp.name), K(cosp.name), K(frcp.name), K(angp.name))
            dma("sp", tabs[l, i, 0], cosp[:, :], [K(cosp.name)], [K("tabs", l)])
            dma("sp", tabs[l, i, 1], sinp[:, :], [K(sinp.name)], [K("tabs", l)])

    for l in range(DEPTH):
        kv = K("vecs", l)
        vec_src = [("caw0", conv_a_w[l, 0]), ("caw1", conv_a_w[l, 1]), ("caw2", conv_a_w[l, 2]), ("caw3", conv_a_w[l, 3]),
                   ("cab", conv_a_b[l]), ("gxb", gate_x_b[l]), ("gab", gate_a_b[l]), ("lam", lru_lambda[l]),
                   ("cbw0", conv_b_w[l, 0]), ("cbw1", conv_b_w[l, 1]), ("cbw2", conv_b_w[l, 2]),
                   ("ssd", ssm_d[l]), ("glb", glu_b[l]), ("l1g", ln1_g[l]), ("l1b", ln1_b[l]), ("l2g", ln2_g[l]), ("l2b", ln2_b[l])]
        vstage = m_f[:, :, :].rearrange("p f n -> p (f n)")[:, 3456:3584]
        sstage = m_f[:, :, :].rearrange("p f n -> p (f n)")[:, 3584:3712]
        nvr = 0
        for name, src in vec_src:
            n = src.shape[0] // 128
            c0 = VEC[name]
            dma("sp", vstage[c0:c0 + n, :], src.rearrange("(a p) -> a p", p=128), (), [K("vstage")])
            nvr = max(nvr, c0 + n)
        psv = next_ps()
        S.add("pe", lambda e, psv=psv, nvr=nvr, vstage=vstage: e.transpose(out=psv[:, 0:nvr], in_=vstage[0:nvr, :], identity=ident[0:nvr, 0:nvr]),
              [K("vstage"), K("ident")], [K("ps", psv.name)])
        cp(vecs[l][:, 0:nvr], psv[:, 0:nvr], [K("ps", psv.name)], [kv], eng="act")
        lam = vecs[l][:, VEC["lam"]:VEC["lam"] + 4]
        tmpv = vecs[l][:, VEC["tmp"]:VEC["tmp"] + 4]
        lcv = vecs[l][:, VEC["lc"]:VEC["lc"] + 4]
        act(tmpv, lam, AF.Exp, [kv], [kv], scale=-1.0)
        act(tmpv, tmpv, AF.Ln, [kv], [kv], bias=ONE)
        ts(lcv, tmpv, -8.0, None, ALU.mult, None, [kv], [kv])
        for (gw_src, gw_dst, nm) in ((gate_x_w, gxw, "gxw"), (gate_a_w, gaw, "gaw")):
            mset(gstage[:], 0.0, [K("gstage")])
            for a in range(4):
                dma("sp", gstage[0:64, a, 0:64], gw_src[l, 2 * a], (), [K("gstage")])
                dma("sp", gstage[64:128, a, 64:128], gw_src[l, 2 * a + 1], (), [K("gstage")])
            cp(gw_dst[l][:], gstage[:], [K("gstage")], [K(nm, l)])
        ks5 = K("s5t", l)
        kc = K("s5c", l)
        are, aim, ldt = s5t[:, 0, :], s5t[:, 1, :], s5t[:, 2, :]
        for j_, src_ in enumerate((ssm_a_re, ssm_a_im, ssm_log_dt)):
            dma("sp", sstage[16 * j_:16 * j_ + 16, :], src_[l].rearrange("(i q) -> i q", q=128), (), [K("sstage")])
        pss = next_ps()
        S.add("pe", lambda e, pss=pss, sstage=sstage: e.transpose(out=pss[:, 0:48], in_=sstage[0:48, :], identity=ident[0:48, 0:48]),
              [K("sstage"), K("ident")], [K("ps", pss.name)])
        cp(s5t[:, 0:3, :], pss[:, 0:48].rearrange("p (k i) -> p k i", k=3), [K("ps", pss.name)], [ks5], eng="act")
        step = s5t[:, 3, :]
        act(step, ldt, AF.Exp, [ks5], [ks5])
        sar, sai = s5t[:, 4, :], s5t[:, 5, :]
        tt(sar, step, are, ALU.mult, [ks5], [ks5])
        tt(sai, step, aim, ALU.mult, [ks5], [ks5])
        rr = s5c[l][:, C_R, :]
        act(rr, sar, AF.Exp, [ks5], [kc])
        sn, cs = s5t[:, 6, :], s5t[:, 7, :]
        sincos(sai, sn, cs, s5t[:, 8, :], s5t[:, 9, :], ks5, ks5, ks5, ks5, ks5)
        ar, ai, nai = s5c[l][:, C_AR, :], s5c[l][:, C_AI, :], s5c[l][:, C_NAI, :]
        tt(ar, rr, cs, ALU.mult, [ks5, kc], [kc])
        tt(ai, rr, sn, ALU.mult, [ks5, kc], [kc])
        ts(nai, ai, -1.0, None, ALU.mult, None, [kc], [kc])
        den, t8, t9, t10 = s5t[:, 8, :], s5t[:, 9, :], s5t[:, 10, :], s5t[:, 11, :]
        tt(den, are, are, ALU.mult, [ks5], [ks5])
        tt(t9, aim, aim, ALU.mult, [ks5], [ks5])
        tt(den, den, t9, ALU.add, [ks5], [ks5])
        S.add("dve", lambda e, den=den: e.reciprocal(out=den, in_=den), [ks5], [ks5])
        nr = s5t[:, 6, :]
        ts(nr, ar, -1.0, None, ALU.add, None, [kc, ks5], [ks5])
        tt(t9, nr, are, ALU.mult, [ks5], [ks5])
        tt(t10, ai, aim, ALU.mult, [ks5, kc], [ks5])
        tt(t9, t9, t10, ALU.add, [ks5], [ks5])
        tt(s5c[l][:, C_CR, :], t9, den, ALU.mult, [ks5], [kc])
        tt(t9, ai, are, ALU.mult, [ks5, kc], [ks5])
        tt(t10, nr, aim, ALU.mult, [ks5], [ks5])
        tt(t9, t9, t10, ALU.subtract, [ks5], [ks5])
        tt(s5c[l][:, C_CI, :], t9, den, ALU.mult, [ks5], [kc])
        ts(s5c[l][:, C_NCI, :], s5c[l][:, C_CI, :], -1.0, None, ALU.mult, None, [kc], [kc])
        cp(s5c[l][:, 7, :], sai, [ks5], [kc])
        kb = K("braw")
        dma("sp", braw[:, 0, :, :], ssm_b_re[l].rearrange("(i q c) -> q i c", q=128, c=16), (), [kb])
        dma("sp", braw[:, 1, :, :], ssm_b_im[l].rearrange("(i q c) -> q i c", q=128, c=16), (), [kb])
        crb = s5c[l][:, C_CR, :].unsqueeze(2).to_broadcast([128, 16, 16])
        cib = s5c[l][:, C_CI, :].unsqueeze(2).to_broadcast([128, 16, 16])
        kbb = K("bbar")
        t3 = tf[0][:, 0:256].rearrange("p (i c) -> p i c", c=16)
        tt(bbar[:, 0], braw[:, 0], crb, ALU.mult, [kb, kc], [kbb])
        tt(t3, braw[:, 1], cib, ALU.mult, [kb, kc], [K(tf[0].name)])
        tt(bbar[:, 0], bbar[:, 0], t3, ALU.subtract, [kbb, K(tf[0].name)], [kbb])
        tt(bbar[:, 1], braw[:, 1], crb, ALU.mult, [kb, kc], [kbb])
        tt(t3, braw[:, 0], cib, ALU.mult, [kb, kc], [K(tf[0].name)])
        tt(bbar[:, 1], bbar[:, 1], t3, ALU.add, [kbb, K(tf[0].name)], [kbb])
        for i in range(16):
            j0 = (2 * i) % 8
            for ri in range(2):
                bpad = bpads[j0 // 2]
                kbp = K("bpad", j0 // 2)
                cp(bpad[0:64, 16 * j0:16 * j0 + 16], bbar[0:64, ri, i, :], [kbb], [kbp])
                cp(bpad[64:128, 16 * (j0 + 1):16 * (j0 + 1) + 16], bbar[64:128, ri, i, :], [kbb], [kbp])
                ps = next_ps()
                S.add("pe", lambda e, ps=ps, bpad=bpad: e.transpose(out=ps[:, 0:128], in_=bpad, identity=ident[:]),
                      [kbp, K("ident")], [K("ps", ps.name)])
                cp(BTs[:, i, ri, :], ps[:, 0:128], [K("ps", ps.name)], [K("BTs")], eng="act")
        kcd = K("cdup")
        for ri, csrc in ((0, ssm_c_re), (1, ssm_c_im)):
            src = csrc[l].rearrange("(a r p) -> r a p", r=128, p=64)
            dma("sp", cdup[:, ri, :, 0:64], src, (), [kcd])
            dma("sp", cdup[:, ri, :, 64:128], src, (), [kcd])
        for ri in range(2):
            for a in range(4):
                ps = next_ps()
                S.add("pe", lambda e, ps=ps, ri=ri, a=a: e.transpose(out=ps[:, 0:128], in_=cdup[:, ri, a, :], identity=ident[:]),
                      [kcd, K("ident")], [K("ps", ps.name)])
                for i in range(4 * a, 4 * a + 4):
                    jj = ((2 * i) % 8) // 2
                    tt(CTs[:, i, ri, :], ps[:, 0:128], masks[:, 2 * jj + ri, :], ALU.mult,
                       [K("ps", ps.name), K("masks")], [K("CTs")])
        bcv = bc_scr[l].rearrange("i p r c -> p i r c")
        dma("sp", bcv[:, :, 0:2, :], BTs, [K("BTs")], [K("bc_scr", l)])
        dma("sp", bcv[:, :, 2:4, :], CTs, [K("CTs")], [K("bc_scr", l)])

    prep_tables(0)

    def layer_norm(l, N, gname, bname):
        r3 = r_f.rearrange("p (f n) -> p f n", f=8)
        rb3 = r_b.rearrange("p (f n) -> p f n", f=8)
        kr = K("r_f")
        sq_b = hid_v
        for f in range(8):
            act(rb3[:, f, 0:N], r3[:, f, 0:N], AF.Copy, [kr], [K("r_b")])
            tt(sq_b[:, f, 0:N], r3[:, f, 0:N], r3[:, f, 0:N], ALU.mult, [kr], [K("hid")], eng=POOLX)
        ps_s = next_ps()
        mm(ps_s, N, [(ones_b[:], rb3[:, f, 0:N]) for f in range(8)], [K("ones_b"), K("r_b")])
        ps_q = next_ps()
        mm(ps_q, N, [(ones_b[:], sq_b[:, f, 0:N]) for f in range(8)], [K("ones_b"), K("hid")])
        mean = tmpf()
        act(mean[:, 0:N], ps_s[:, 0:N], AF.Copy, [K("ps", ps_s.name)], [K(mean.name)], scale=1.0 / D)
        msq = tmpf()
        tt(msq[:, 0:N], mean[:, 0:N], mean[:, 0:N], ALU.mult, [K(mean.name)], [K(msq.name)])
        var = tmpf()
        stt(var[:, 0:N], ps_q[:, 0:N], 1.0 / D, msq[:, 0:N], ALU.mult, ALU.subtract, [K("ps", ps_q.name), K(msq.name)], [K(var.name)])
        act(var[:, 0:N], var[:, 0:N], AF.Ln, [K(var.name)], [K(var.name)], bias=EPS)
        rstd = tmpf()
        act(rstd[:, 0:N], var[:, 0:N], AF.Exp, [K(var.name)], [K(rstd.name)], scale=-0.5)
        tpair = [tmpf(), tmpf()]
        for f in range(8):
            t = tpair[f % 2]
            tt(t[:, 0:N], r3[:, f, 0:N], mean[:, 0:N], ALU.subtract, [kr, K(mean.name)], [K(t.name)], eng=POOLX)
            tt(t[:, 0:N], t[:, 0:N], rstd[:, 0:N], ALU.mult, [K(t.name), K(rstd.name)], [K(t.name)])
            act(x_f[:, f, 0:N], t[:, 0:N], AF.Identity, [K(t.name), K("vecs", l)], [K("x_f", f)],
                bias=V(l, bname, f), scale=V(l, gname, f))
            act(x_b[:, f, 0:N], t[:, 0:N], AF.Identity, [K(t.name), K("vecs", l)], [K("x_b", f)],
                bias=V(l, bname, f), scale=V(l, gname, f))

    def merge_loads_g(l, br, proj_w):
        wp = yield ("load", w_rows(proj_w[l], 0, 512, 0, D), 4, D)
        wg = []
        for half in range(2):
            w_ = yield ("load", w_rows(w_in[l], 0, D, 3072 + 1024 * br + 512 * half, 512), 8, 512)
            wg.append(w_)
        return wp, wg

    def proj_merge_gen(l, N, br, vkey, loads):
        (wp, kwp), wgs = loads
        for half in range(2):
            wg, kwg = wgs[half]
            for ff in range(4):
                f = 4 * half + ff
                ps_g = next_ps()
                mm(ps_g, N, [(wg[:, kt, ff * 128:(ff + 1) * 128], x_b[:, kt, 0:N]) for kt in range(8)],
                   [kwg] + [K("x_b", kt) for kt in range(8)])
                g = tmpf()
                act(g[:, 0:N], ps_g[:, 0:N], AF.Sigmoid, [K("ps", ps_g.name)], [K(g.name)])
                ps_o = next_ps()
                mm(ps_o, N, [(wp[:, kt, f * 128:(f + 1) * 128], v_b[:, kt, 0:N]) for kt in range(4)], [kwp, vkey])
                if br == 0:
                    tt(m_f[:, f, 0:N], ps_o[:, 0:N], g[:, 0:N], ALU.mult, [K("ps", ps_o.name), K(g.name)], [K("m_f", f)])
                else:
                    tt(g[:, 0:N], ps_o[:, 0:N], g[:, 0:N], ALU.mult, [K("ps", ps_o.name), K(g.name)], [K(g.name)])
                    if br == 1:
                        tt(m_f[:, f, 0:N], m_f[:, f, 0:N], g[:, 0:N], ALU.add, [K("m_f", f), K(g.name)], [K("m_f", f)], eng=S5ADD)
                    else:
                        tt(m_b[:, f, 0:N], m_f[:, f, 0:N], g[:, 0:N], ALU.add, [K("m_f", f), K(g.name)], [K("m_b", f)], eng=S5ADD)
            yield

    def layer_pass(l, N, nseq, L, prompt, last):
        kxb = [K("x_b", kt) for kt in range(8)]
        kv = K("vecs", l)
        sample = not prompt
        deferred = []

        def seqv(ap2d):
            return ap2d.rearrange("p (s t) -> p s t", t=L)

        PW = L + 3
        upA = upad[:, 0:4 * nseq * PW].rearrange("p (a s w) -> p a s w", a=4, s=nseq)
        xc3 = xc_f.rearrange("p (a n) -> p a n", a=4)
        xcb3 = xc_b.rearrange("p (a n) -> p a n", a=4)
        kup = K("upad")
        if sample:
            ksst = K("sst")
            srcc = st_conv[l].rearrange("s k (a p) -> (s k a) p", p=128)
            dma("sp", sst[0:120, 0, :], srcc[0:120], (), [ksst])
            dma("sp", sst[0:72, 1, :], srcc[120:192], (), [ksst])
            dma("sp", sst[0:64, 2, :], st_h[l].rearrange("s (a p) -> (s a) p", p=128), (), [ksst])
            dma("sp", sst[:, 3, :], st_sc[l].rearrange("s k (a p) -> (s k a) p", p=128), (), [ksst])
            srcr = st_re[l].rearrange("s (i q) -> (s i) q", q=128)
            srci = st_im[l].rearrange("s (i q) -> (s i) q", q=128)
            for hh in range(2):
                dma("sp", sst[:, 4 + hh, :], srcr[128 * hh:128 * hh + 128], (), [ksst])
                dma("sp", sst[:, 6 + hh, :], srci[128 * hh:128 * hh + 128], (), [ksst])
            nrows = [120, 72, 64, 128, 128, 128, 128, 128]
            for j in range(8):
                nr = nrows[j]
                ps = next_ps()
                S.add("pe", lambda e, ps=ps, j=j, nr=nr: e.transpose(out=ps[:, 0:nr], in_=sst[0:nr, j, :], identity=ident[0:nr, 0:nr]),
                      [ksst, K("ident")], [K("ps", ps.name)])
                kps = K("ps", ps.name)
                if j == 0:
                    cp(upA[:, :, 0:10, 0:3].rearrange("p a s k -> p s k a"), ps[:, 0:120].rearrange("p (s k a) -> p s k a", k=3, a=4),
                       [kps], [kup, K("r_f"), K("r_b")])
                elif j == 1:
                    cp(upA[:, :, 10:16, 0:3].rearrange("p a s k -> p s k a"), ps[:, 0:72].rearrange("p (s k a) -> p s k a", k=3, a=4),
                       [kps], [kup, K("r_f"), K("r_b")])
                elif j == 2:
                    cp(s_h0[:, :, :].rearrange("p a s -> p s a"), ps[:, 0:64].rearrange("p (s a) -> p s a", a=4), [kps], [K("s_h0")])
                elif j == 3:
                    cp(s_sc0[:, :, :, :].rearrange("p a s k -> p s k a"), ps[:, 0:128].rearrange("p (s k a) -> p s k a", k=2, a=4), [kps], [K("s_sc0")])
                elif j in (4, 5):
                    hh = j - 4
                    cp(s_re0[:, :, 8 * hh:8 * hh + 8].rearrange("q i s -> q s i"), ps[:, 0:128].rearrange("q (s i) -> q s i", i=16), [kps], [K("s_re0")])
                else:
                    hh = j - 6
                    cp(s_im0[:, :, 8 * hh:8 * hh + 8].rearrange("q i s -> q s i"), ps[:, 0:128].rearrange("q (s i) -> q s i", i=16), [kps], [K("s_im0")])
        PWB = L + 2
        upB = upad[:, 0:4 * nseq * PWB].rearrange("p (a s w) -> p a s w", a=4, s=nseq)

        us3 = us_f.rearrange("p (a n) -> p a n", a=4)
        usb3 = us_b.rearrange("p (a n) -> p a n", a=4)
        wus, kwus = yield ("load", w_rows(w_in[l], 0, D, 2560, 512), 8, 512)
        for a in range(4):
            ps = next_ps()
            mm(ps, N, [(wus[:, kt, a * 128:(a + 1) * 128], x_b[:, kt, 0:N]) for kt in range(8)], [kwus] + kxb)
            cp(us3[:, a, 0:N], ps[:, 0:N], [K("ps", ps.name)], [K("us_f", a)], eng="act")
            cp(usb3[:, a, 0:N], ps[:, 0:N], [K("ps", ps.name)], [K("us_b", a)], eng="act")
        kc = K("s5c", l)
        ps_y = psum[7] if prompt else psum[6]

        def s5_B(i):
            a = i // 4
            jb = i % NBC
            bc = bcbuf[jb]
            dma("sp", bc[:, 0:4, :], bc_scr[l, i], [K("bc_scr", l)], [K("bcbuf", jb)])
            act(bc[:, 4, :], bc[:, 2, :], AF.Copy, [K("bcbuf", jb)], [K("bcbuf", jb)], scale=-1.0)
            ps_r = next_ps()
            mm(ps_r, N, [(bc[:, 0, :], usb3[:, a, 0:N])], [K("bcbuf", jb), K("us_b", a)])
            ps_m = next_ps()
            mm(ps_m, N, [(bc[:, 1, :], usb3[:, a, 0:N])], [K("bcbuf", jb), K("us_b", a)])
            return ps_r, ps_m

        def s5_body(i, ps_r, ps_m):
            kpr, kpm = K("ps", ps_r.name), K("ps", ps_m.name)
            if prompt:
                Gr = next_ps()
                kGr = K("ps", Gr.name)
                Gi = tmpf()
                kGi = K(Gi.name)
                jc, js = (2 * i) % NTB, (2 * i + 1) % NTB
                tC, tS = tabbuf[jc], tabbuf[js]
                kct, kst = K("tabbuf", jc), K("tabbuf", js)
                dma("sp", tC[:, 0:N], tabs[l, i, 0][:, 0:N], [K("tabs", l)], [kct])
                dma("sp", tS[:, 0:N], tabs[l, i, 1][:, 0:N], [K("tabs", l)], [kst])
                cosb, sinb = tC[:, 0:N], tS[:, 0:N]
                bm = tmpf()
                cp(bm[:, 0:N], ps_m[:, 0:N], [kpm], [K(bm.name)], eng="act")
                t1, t2 = tmpf(), tmpf()
                tt(t1[:, 0:N], ps_r[:, 0:N], cosb, ALU.mult, [kpr, kct], [K(t1.name)])
                tt(t2[:, 0:N], bm[:, 0:N], sinb, ALU.mult, [K(bm.name), kst], [K(t2.name)], eng=S5ADD)
                tt(t1[:, 0:N], t1[:, 0:N], t2[:, 0:N], ALU.add, [K(t1.name), K(t2.name)], [K(t1.name)], eng=S5ADD)
                t3_, t4 = tmpf(), tmpf()
                tt(t3_[:, 0:N], ps_m[:, 0:N], cosb, ALU.mult, [kpm, kct], [K(t3_.name)])
                tt(t4[:, 0:N], ps_r[:, 0:N], sinb, ALU.mult, [kpr, kst], [K(t4.name)])
                tt(t3_[:, 0:N], t3_[:, 0:N], t4[:, 0:N], ALU.subtract, [K(t3_.name), K(t4.name)], [K(t3_.name)], eng=S5ADD)
                rbc = s5c[l][:, C_R, i:i + 1].to_broadcast([128, N])
                ksh = K("ssm_h", l, i)
                cN = tC[:, N - 1:N]
                sN = tS[:, N - 1:N]
                S.add("dve", lambda e, Gr=Gr, t1=t1, i=i, rbc=rbc: e.tensor_tensor_scan(
                    out=Gr[:, 0:N], data0=rbc, data1=t1[:, 0:N], initial=ssm_h[l][:, i, 0:1], op0=ALU.mult, op1=ALU.add),
                    [K(t1.name), kc, ksh], [kGr], dur=0.12 + 2 * N / 960.0)
                S.add("dve", lambda e, Gi=Gi, t3_=t3_, i=i, rbc=rbc: e.tensor_tensor_scan(
                    out=Gi[:, 0:N], data0=rbc, data1=t3_[:, 0:N], initial=ssm_h[l][:, i, 1:2], op0=ALU.mult, op1=ALU.add),
                    [K(t3_.name), kc, ksh], [kGi], dur=0.12 + 2 * N / 960.0)
                sm = small[:, 0:2]
                ts(sm[:, 0:1], Gi[:, N - 1:N], sN, None, ALU.mult, None, [kGi, kst], [K("small")])
                ts(sm[:, 1:2], Gr[:, N - 1:N], sN, None, ALU.mult, None, [kGr, kst], [K("small")])
                stt(ssm_h[l][:, i, 0:1], Gr[:, N - 1:N], cN, sm[:, 0:1], ALU.mult, ALU.subtract, [kGr, kct, K("small")], [ksh])
                stt(ssm_h[l][:, i, 1:2], Gi[:, N - 1:N], cN, sm[:, 1:2], ALU.mult, ALU.add, [kGi, kct, K("small")], [ksh])
                u1, u2, u3, u4 = tmpb(), tmpb(), tmpb(), tmpb()
                tt(u1[:, 0:N], Gr[:, 0:N], cosb, ALU.mult, [kGr, kct], [K(u1.name)])
                tt(u4[:, 0:N], Gr[:, 0:N], sinb, ALU.mult, [kGr, kst], [K(u4.name)])
                tt(u2[:, 0:N], Gi[:, 0:N], sinb, ALU.mult, [kGi, kst], [K(u2.name)], eng=S5ADD)
                tt(u3[:, 0:N], Gi[:, 0:N], cosb, ALU.mult, [kGi, kct], [K(u3.name)], eng=S5ADD)
                return (u1, u2, u3, u4)
            hb_r, hb_i = tmpb(), tmpb()
            if True:
                Gr, Gi = tmpf(), tmpf()
                kGr, kGi = K(Gr.name), K(Gi.name)
                arp = s5c[l][:, C_AR, i:i + 1]
                aip = s5c[l][:, C_AI, i:i + 1]
                naip = s5c[l][:, C_NAI, i:i + 1]
                u1 = small[:, 0:NSS]
                u2 = small[:, NSS:2 * NSS]
                for t in range(L):
                    pr = s_re0[:, i, :] if t == 0 else Gr[:, t - 1:N:L]
                    pi_ = s_im0[:, i, :] if t == 0 else Gi[:, t - 1:N:L]
                    stt(u1, pi_, naip, ps_r[:, t:N:L], ALU.mult, ALU.add, [kGi, K("s_im0"), kc, kpr], [K("small")])
                    stt(u2, pr, aip, ps_m[:, t:N:L], ALU.mult, ALU.add, [kGr, K("s_re0"), kc, kpm], [K("small")])
                    stt(Gr[:, t:N:L], pr, arp, u1, ALU.mult, ALU.add, [kGr, K("s_re0"), kc, K("small")], [kGr])
                    stt(Gi[:, t:N:L], pi_, arp, u2, ALU.mult, ALU.add, [kGi, K("s_im0"), kc, K("small")], [kGi])
                cp(stg_o[:, 4:6, i:128:16], Gr[:, L - 1:N:L].rearrange("p (h s) -> p h s", h=2), [kGr], [K("stg_o")])
                cp(stg_o[:, 6:8, i:128:16], Gi[:, L - 1:N:L].rearrange("p (h s) -> p h s", h=2), [kGi], [K("stg_o")])
                cp(hb_r[:, 0:N], Gr[:, 0:N], [kGr], [K(hb_r.name)], eng="act")
                cp(hb_i[:, 0:N], Gi[:, 0:N], [kGi], [K(hb_i.name)], eng="act")
            return hb_r, hb_i

        def s5_C(i, *hb):
            ii = i % 4
            jb = i % NBC
            bc = bcbuf[jb]
            if len(hb) == 4:
                terms = [(2, hb[0]), (4, hb[1]), (3, hb[2]), (3, hb[3])]
            else:
                terms = [(2, hb[0]), (3, hb[1])]
            nt_ = len(terms)
            for j_, (slot, hbt) in enumerate(terms):
                S.add("pe", lambda e, slot=slot, hbt=hbt, j_=j_: e.matmul(ps_y[:, 0:N], lhsT=bc[:, slot, :], rhs=hbt[:, 0:N],
                                                                     start=(ii == 0 and j_ == 0), stop=(ii == 3 and j_ == nt_ - 1)),
                      [K("bcbuf", jb), K(hbt.name)], [K("ps", ps_y.name)], dur=0.015 + N / 2150.0)

        def s5_epi(a):
            yt = tmpf()
            stt(yt[:, 0:N], us3[:, a, 0:N], V(l, "ssd", a), ps_y[:, 0:N], ALU.mult, ALU.add, [K("us_f", a), kv, K("ps", ps_y.name)], [K(yt.name)])
            act(zc_t[:, a, 0:N], yt[:, 0:N], AF.Gelu_apprx_tanh, [K(yt.name)], [K("zc", a)])


        lA0 = yield ("load", w_rows(w_in[l], 0, D, 0, 512), 8, 512)
        lA1 = yield ("load", w_rows(w_in[l], 0, D, 512, 512), 8, 512)
        loadsA = (lA0, lA1)
        post = {}

        def gen_units():
            (wxa, kwxa), (wya, kwya) = loadsA
            loads_mA = yield from merge_loads_g(l, 0, proj_a)
            for a in range(4):
                if prompt:
                    cp(upA[:, a, 0, 0:3], convA_st[l][:, a, :], [K("convA_st", l)], [kup], eng="act")
                ps = next_ps()
                mm(ps, N, [(wxa[:, kt, a * 128:(a + 1) * 128], x_b[:, kt, 0:N]) for kt in range(8)], [kwxa] + kxb)
                cp(upA[:, a, :, 3:3 + L], seqv(ps[:, 0:N]), [K("ps", ps.name)], [kup], eng="act")
                xo = seqv(xc3[:, a, 0:N])
                kxc = K("xc_f", a)
                ts(xo, upA[:, a, :, 0:L], V(l, "caw0", a), V(l, "cab", a), ALU.mult, ALU.add, [kup, kv], [kxc])
                for k in range(1, 4):
                    stt(xo, upA[:, a, :, k:k + L], V(l, "caw%d" % k, a), xo, ALU.mult, ALU.add, [kup, kv, kxc], [kxc])
                if prompt:
                    cp(convA_st[l][:, a, :], upA[:, a, 0, L:L + 3], [kup], [K("convA_st", l)], eng="act")
                    if last:
                        dma("sp", o_conv_p[l].rearrange("k (a p) -> p a k", p=128)[:, a], upA[:, a, 0, L:L + 3], [kup], (), output=True)
                else:
                    cp(stg_o[:, 0, 0:120].rearrange("p (s k a) -> p s k a", k=3, a=4)[:, :, :, a], upA[:, a, 0:10, L:L + 3], [kup], [K("stg_o")])
                    cp(stg_o[:, 1, 0:72].rearrange("p (s k a) -> p s k a", k=3, a=4)[:, :, :, a], upA[:, a, 10:16, L:L + 3], [kup], [K("stg_o")])
                cp(xcb3[:, a, 0:N], xc3[:, a, 0:N], [kxc], [K("xc_b", a)], eng="act")
                ps_gx = next_ps()
                mm(ps_gx, N, [(gxw[l][:, a, :], xcb3[:, a, 0:N])], [K("gxw", l), K("xc_b", a)])
                gx = tmpf()
                act(gx[:, 0:N], ps_gx[:, 0:N], AF.Sigmoid, [K("ps", ps_gx.name), kv], [K(gx.name)], bias=V(l, "gxb", a))
                ps_ga = next_ps()
                mm(ps_ga, N, [(gaw[l][:, a, :], xcb3[:, a, 0:N])], [K("gaw", l), K("xc_b", a)])
                at = tmpf()
                act(at[:, 0:N], ps_ga[:, 0:N], AF.Sigmoid, [K("ps", ps_ga.name), kv], [K(at.name)], bias=V(l, "gab", a))
                act(at[:, 0:N], at[:, 0:N], AF.Exp, [K(at.name), kv], [K(at.name)], scale=V(l, "lc", a))
                ml = tmpf()
                tt(ml[:, 0:N], at[:, 0:N], at[:, 0:N], ALU.mult, [K(at.name)], [K(ml.name)], eng=POOLX)
                act(ml[:, 0:N], ml[:, 0:N], AF.Sqrt, [K(ml.name)], [K(ml.name)], bias=ONE, scale=-1.0)
                tt(gx[:, 0:N], gx[:, 0:N], xc3[:, a, 0:N], ALU.mult, [K(gx.name), kxc], [K(gx.name)], eng=POOLX)
                tt(gx[:, 0:N], gx[:, 0:N], ml[:, 0:N], ALU.mult, [K(gx.name), K(ml.name)], [K(gx.name)])
                h = tmpf()
                kh = K(h.name)
                if prompt:
                    S.add("dve", lambda e, h=h, at=at, gx=gx, a=a: e.tensor_tensor_scan(
                        out=h[:, 0:N], data0=at[:, 0:N], data1=gx[:, 0:N], initial=lru_h[l][:, a:a + 1], op0=ALU.mult, op1=ALU.add),
                        [K(at.name), K(gx.name), K("lru_h", l)], [kh], dur=0.12 + 2 * N / 960.0)
                    cp(lru_h[l][:, a:a + 1], h[:, N - 1:N], [kh], [K("lru_h", l)])
                    if last:
                        dma("sp", o_h_p[l].rearrange("(a p) -> p a", p=128)[:, a:a + 1], h[:, N - 1:N], [kh], (), output=True)
                else:
                    for t in range(L):
                        prev = s_h0[:, a, :] if t == 0 else h[:, t - 1:N:L]
                        tt(h[:, t:N:L], at[:, t:N:L], prev, ALU.mult, [K(at.name), K("s_h0"), kh], [kh])
                        tt(h[:, t:N:L], h[:, t:N:L], gx[:, t:N:L], ALU.add, [kh, K(gx.name)], [kh])
                    cp(stg_o[:, 2, a:64:4], h[:, L - 1:N:L], [kh], [K("stg_o")])
                ps_ya = next_ps()
                mm(ps_ya, N, [(wya[:, kt, a * 128:(a + 1) * 128], x_b[:, kt, 0:N]) for kt in range(8)], [kwya] + kxb)
                gl = tmpf()
                act(gl[:, 0:N], ps_ya[:, 0:N], AF.Gelu_apprx_tanh, [K("ps", ps_ya.name)], [K(gl.name)])
                tt(v_b[:, a, 0:N], h[:, 0:N], gl[:, 0:N], ALU.mult, [kh, K(gl.name)], [K("v_b")], eng=POOLX)
                yield
            loadsB = []
            for j in range(3):
                w_ = yield ("load", w_rows(w_in[l], 0, D, 1024 + 512 * j, 512), 8, 512)
                loadsB.append(w_)
            for _ in proj_merge_gen(l, N, 0, K("v_b"), loads_mA):
                yield
            (wsb, kwsb), (wsc, kwsc), (wsh, kwsh) = loadsB
            if sample:
                cp(upB[:, :, :, 0:2], s_sc0[:, :, :, :], [K("s_sc0")], [kup, K("r_f"), K("r_b")])
            loads_mB = yield from merge_loads_g(l, 1, proj_b)
            for a in range(4):
                if prompt:
                    cp(upB[:, a, 0, 0:2], convB_st[l][:, a, :], [K("convB_st", l)], [kup], eng="act")
                ps_c = next_ps()
                mm(ps_c, N, [(wsc[:, kt, a * 128:(a + 1) * 128], x_b[:, kt, 0:N]) for kt in range(8)], [kwsc] + kxb)
                sct = tmpf()
                cp(sct[:, 0:N], ps_c[:, 0:N], [K("ps", ps_c.name)], [K(sct.name)], eng="act")
                ps_h = next_ps()
                mm(ps_h, N, [(wsh[:, kt, a * 128:(a + 1) * 128], x_b[:, kt, 0:N]) for kt in range(8)], [kwsh] + kxb)
                tt(upB[:, a, :, 2:2 + L], seqv(ps_h[:, 0:N]), seqv(sct[:, 0:N]), ALU.mult, [K("ps", ps_h.name), K(sct.name)], [kup])
                cu = tmpf()
                cuo = seqv(cu[:, 0:N])
                ts(cuo, upB[:, a, :, 0:L], V(l, "cbw0", a), None, ALU.mult, None, [kup, kv], [K(cu.name)])
                for k in range(1, 3):
                    stt(cuo, upB[:, a, :, k:k + L], V(l, "cbw%d" % k, a), cuo, ALU.mult, ALU.add, [kup, kv, K(cu.name)], [K(cu.name)])
                if prompt:
                    cp(convB_st[l][:, a, :], upB[:, a, 0, L:L + 2], [kup], [K("convB_st", l)], eng="act")
                    if last:
                        dma("sp", o_sc_p[l].rearrange("k (a p) -> p a k", p=128)[:, a], upB[:, a, 0, L:L + 2], [kup], (), output=True)
                else:
                    cp(stg_o[:, 3, 0:128].rearrange("p (s k a) -> p s k a", k=2, a=4)[:, :, :, a], upB[:, a, :, L:L + 2], [kup], [K("stg_o")])
                ps_b = next_ps()
                mm(ps_b, N, [(wsb[:, kt, a * 128:(a + 1) * 128], x_b[:, kt, 0:N]) for kt in range(8)], [kwsb] + kxb)
                tt(v_b[:, a, 0:N], ps_b[:, 0:N], cu[:, 0:N], ALU.mult, [K("ps", ps_b.name), K(cu.name)], [K("v_b")])
                yield
            post["glu"] = yield ("load", w_rows(glu_w[l], 0, 512, 0, 512), 4, 512)
            post["mC"] = yield from merge_loads_g(l, 2, proj_c)
            for _ in proj_merge_gen(l, N, 1, K("v_b"), loads_mB):
                yield

        units = gen_units()
        units_alive = [True]

        def step_units():
            try:
                r = next(units)
                while r is not None:
                    v = yield r
                    r = units.send(v)
            except StopIteration:
                units_alive[0] = False
        def s5_sample_batched():
            psr, psm = [], []
            for h_ in range(2):
                pr_, pm_ = next_ps(), next_ps()
                psr.append(pr_)
                psm.append(pm_)
                for j_ in range(8):
                    i = 8 * h_ + j_
                    a = i // 4
                    jb = i % NBC
                    bc = bcbuf[jb]
                    dma("sp", bc[:, 0:4, :], bc_scr[l, i], [K("bc_scr", l)], [K("bcbuf", jb)])
                    for (ps_, slot) in ((pr_, 0), (pm_, 1)):
                        S.add("pe", lambda e, ps_=ps_, slot=slot, bc=bc, j_=j_, a=a: e.matmul(ps_[:, 64 * j_:64 * j_ + 64], lhsT=bc[:, slot, :], rhs=usb3[:, a, 0:N],
                                                                                      start=True, stop=True),
                              [K("bcbuf", jb), K("us_b", a)], [K("ps", ps_.name)], dur=0.06)
            Hr = [tmpf(), tmpf()]
            Hi = [tmpf(), tmpf()]
            u1, u2 = tmpf(), tmpf()
            v3 = lambda t_, col0: t_[:, col0:512:4].rearrange("p (j s) -> p j s", s=NSS)
            u1v = u1[:, 0:128].rearrange("p (j s) -> p j s", s=NSS)
            u2v = u2[:, 0:128].rearrange("p (j s) -> p j s", s=NSS)
            for t in range(L):
                for h_ in range(2):
                    sl = slice(8 * h_, 8 * h_ + 8)
                    bcst = lambda c_: s5c[l][:, c_, sl].unsqueeze(2).to_broadcast([128, 8, NSS])
                    pr = s_re0[:, sl, :] if t == 0 else v3(Hr[h_], t - 1)
                    pi_ = s_im0[:, sl, :] if t == 0 else v3(Hi[h_], t - 1)
                    kHr, kHi = K(Hr[h_].name), K(Hi[h_].name)
                    kpr, kpm = K("ps", psr[h_].name), K("ps", psm[h_].name)
                    rd = [kHr, kHi, K("s_re0"), K("s_im0"), kc]
                    tt(u1v, pi_, bcst(C_NAI), ALU.mult, rd, [K(u1.name)])
                    tt(u1v, v3(psr[h_], t), u1v, ALU.add, [kpr, K(u1.name)], [K(u1.name)])
                    tt(u2v, pr, bcst(C_AI), ALU.mult, rd, [K(u2.name)])
                    tt(u2v, v3(psm[h_], t), u2v, ALU.add, [kpm, K(u2.name)], [K(u2.name)])
                    tt(v3(Hr[h_], t), pr, bcst(C_AR), ALU.mult, rd, [kHr])
                    tt(v3(Hr[h_], t), v3(Hr[h_], t), u1v, ALU.add, [kHr, K(u1.name)], [kHr])
                    tt(v3(Hi[h_], t), pi_, bcst(C_AR), ALU.mult, rd, [kHi])
                    tt(v3(Hi[h_], t), v3(Hi[h_], t), u2v, ALU.add, [kHi, K(u2.name)], [kHi])
            hbs = []
            for h_ in range(2):
                for hs in range(2):
                    for (H_, base) in ((Hr[h_], 4), (Hi[h_], 6)):
                        src_ = H_[:, L - 1:512:4].rearrange("p (j s) -> p j s", s=NSS)[:, :, 8 * hs:8 * hs + 8].rearrange("p j s -> p s j")
                        dst_ = stg_o[:, base + hs, :].rearrange("p (s i) -> p s i", i=16)[:, :, 8 * h_:8 * h_ + 8]
                        cp(dst_, src_, [K(H_.name)], [K("stg_o")])
                br_, bi_ = tmpb(), tmpb()
                cp(br_[:, :], Hr[h_][:, :], [K(Hr[h_].name)], [K(br_.name)], eng="act")
                cp(bi_[:, :], Hi[h_][:, :], [K(Hi[h_].name)], [K(bi_.name)], eng="act")
                hbs.append((br_, bi_))
            for i in range(16):
                h_, j_ = i // 8, i % 8
                jb = i % NBC
                bc = bcbuf[jb]
                dma("sp", bc[:, 0:4, :], bc_scr[l, i], [K("bc_scr", l)], [K("bcbuf", jb)])
                ii = i % 4
                for (slot, hbt, first, lastm) in ((2, hbs[h_][0], True, False), (3, hbs[h_][1], False, True)):
                    S.add("pe", lambda e, slot=slot, hbt=hbt, j_=j_, ii=ii, first=first, lastm=lastm, bc=bc: e.matmul(
                        ps_y[:, 0:N], lhsT=bc[:, slot, :], rhs=hbt[:, 64 * j_:64 * j_ + 64], start=(ii == 0 and first), stop=(ii == 3 and lastm)),
                        [K("bcbuf", jb), K(hbt.name)], [K("ps", ps_y.name)], dur=0.06)
                if ii == 3:
                    s5_epi(i // 4)

        if sample and dbg.get('sbatch', True):
            s5_sample_batched()
            cur = None
        else:
            cur = s5_B(0)
        for i in (range(16) if cur is not None else ()):
            hb = s5_body(i, *cur)
            s5_C(i, *hb)
            if i % 4 == 3:
                s5_epi(i // 4)
            if i >= 1 and units_alive[0]:
                yield from step_units()
            cur = s5_B(i + 1) if i + 1 < 16 else None
        while units_alive[0]:
            yield from step_units()
        if prompt and last:
            kall = [K("ssm_h", l, i) for i in range(16)]
            dma("sp", o_re_p[l].rearrange("(i q) -> q i", q=128), ssm_h[l][:, :, 0], kall, (), output=True)
            dma("sp", o_im_p[l].rearrange("(i q) -> q i", q=128), ssm_h[l][:, :, 1], kall, (), output=True)
        if sample:
            nrows = [120, 72, 64, 128, 128, 128, 128, 128]
            dsts = [o_conv_s[l].rearrange("s k (a p) -> (s k a) p", p=128)[0:120], o_conv_s[l].rearrange("s k (a p) -> (s k a) p", p=128)[120:192],
                    o_h_s[l].rearrange("s (a p) -> (s a) p", p=128), o_sc_s[l].rearrange("s k (a p) -> (s k a) p", p=128),
                    o_re_s[l].rearrange("s (i q) -> (s i) q", q=128)[0:128], o_re_s[l].rearrange("s (i q) -> (s i) q", q=128)[128:256],
                    o_im_s[l].rearrange("s (i q) -> (s i) q", q=128)[0:128], o_im_s[l].rearrange("s (i q) -> (s i) q", q=128)[128:256]]
            for j in range(8):
                nr = nrows[j]
                ps = next_ps()
                S.add("pe", lambda e, ps=ps, j=j, nr=nr: e.transpose(out=ps[0:nr, 0:128], in_=stg_o[:, j, 0:nr], identity=ident[:, :]),
                      [K("stg_o"), K("ident")], [K("ps", ps.name)])
                cp(out_rows[0:nr, j, :], ps[0:nr, 0:128], [K("ps", ps.name)], [K("sst")], eng="act")
                dma("sp", dsts[j], out_rows[0:nr, j, :], [K("sst")], (), output=True)
        chk('C')
        wgu, kwgu = post["glu"]
        for a in range(4):
            ps = next_ps()
            mm(ps, N, [(wgu[:, kt, a * 128:(a + 1) * 128], zc_t[:, kt, 0:N]) for kt in range(4)], [kwgu] + [K("zc", kt) for kt in range(4)])
            sg = tmpf()
            act(sg[:, 0:N], ps[:, 0:N], AF.Sigmoid, [K("ps", ps.name), kv], [K(sg.name)], bias=V(l, "glb", a))
            tt(v_b[:, a, 0:N], zc_t[:, a, 0:N], sg[:, 0:N], ALU.mult, [K("zc", a), K(sg.name)], [K("v_b")])
        for _ in proj_merge_gen(l, N, 2, K("v_b"), post["mC"]):
            pass
        chk('mC')

        r3 = r_f.rearrange("p (f n) -> p f n", f=8)
        kr = K("r_f")
        alias_keys = [kup] + [K("xc_f", a) for a in range(4)] + [K("us_f", a) for a in range(4)] + \
                     [K("xc_b", a) for a in range(4)] + [K("us_b", a) for a in range(4)]
        for half in range(2):
            wo, kwo = yield ("load", w_rows(w_out[l], 0, D, 512 * half, 512), 8, 512)
            for ff in range(4):
                f = 4 * half + ff
                ps = next_ps()
                mm(ps, N, [(wo[:, kt, ff * 128:(ff + 1) * 128], m_b[:, kt, 0:N]) for kt in range(8)], [kwo] + [K("m_b", kt) for kt in range(8)])
                stt(r3[:, f, 0:N], x_f[:, f, 0:N], ALPHA, ps[:, 0:N], ALU.mult, ALU.add, [K("x_f", f), K("ps", ps.name)], [kr] + alias_keys)
        chk('W')
        layer_norm(l, N, "l1g", "l1b")
        chk('LN1')

        for half in range(2):
            for q in range(4):
                wu, kwu = yield ("load", w_rows(mlp_up[l], 0, D, 2048 * half + 512 * q, 512), 8, 512)
                for jj in range(4):
                    j = 4 * q + jj
                    ps = next_ps()
                    mm(ps, N, [(wu[:, kt, jj * 128:(jj + 1) * 128], x_b[:, kt, 0:N]) for kt in range(8)], [kwu] + kxb)
                    rl = tmpf()
                    act(rl[:, 0:N], ps[:, 0:N], AF.Relu, [K("ps", ps.name)], [K(rl.name)])
                    act(hid_v[:, j, 0:N], rl[:, 0:N], AF.Square, [K(rl.name)], [K("hid")])
            for ch in range(2):
                wd0, kwd0 = yield ("load", w_rows(mlp_down[l], 2048 * half, 1024, 512 * ch, 512), 8, 512)
                wd1, kwd1 = yield ("load", w_rows(mlp_down[l], 2048 * half + 1024, 1024, 512 * ch, 512), 8, 512)
                for ff in range(4):
                    f = 4 * ch + ff
                    ps = next_ps()
                    pairs = [(wd0[:, kt, ff * 128:(ff + 1) * 128], hid_v[:, kt, 0:N]) for kt in range(8)] + \
                            [(wd1[:, kt, ff * 128:(ff + 1) * 128], hid_v[:, 8 + kt, 0:N]) for kt in range(8)]
                    mm(ps, N, pairs, [kwd0, kwd1, K("hid")])
                    if half == 0:
                        stt(r3[:, f, 0:N], x_f[:, f, 0:N], ALPHA, ps[:, 0:N], ALU.mult, ALU.add, [K("x_f", f), K("ps", ps.name)], [kr])
                    else:
                        tt(r3[:, f, 0:N], r3[:, f, 0:N], ps[:, 0:N], ALU.add, [kr, K("ps", ps.name)], [kr])
        chk('M')
        layer_norm(l, N, "l2g", "l2b")

    def load_x(C, src, t0, N):
        activate(C)
        ntt = (N + 127) // 128
        if ntt == 4:
            stg = [(us_f[:, 0:1024], [K("us_f", 0), K("us_f", 1)]), (us_f[:, 1024:2048], [K("us_f", 2), K("us_f", 3)]),
                   (us_b[:, :].bitcast(F32), [K("us_b", a_) for a_ in range(4)]),
                   (zc_t[:, :, :].rearrange("p a n -> p (a n)").bitcast(F32), [K("zc", a_) for a_ in range(4)])]
            for tk in range(4):
                dma("sp", stg[tk][0], src[t0 + 128 * tk:t0 + 128 * tk + 128, :], (), stg[tk][1])
            for f in range(8):
                ps = next_ps()
                for tk in range(4):
                    S.add("pe", lambda e, ps=ps, tk=tk, f=f, st=stg[tk][0]: e.transpose(out=ps[:, 128 * tk:128 * tk + 128], in_=st[:, 128 * f:128 * f + 128],
                                                                                 identity=ident[:, :]),
                          stg[tk][1] + [K("ident")], [K("ps", ps.name)])
                cp(x_b[:, f, 0:N], ps[:, 0:N], [K("ps", ps.name)], [K("x_b", f)], eng="dve")
        xin3 = xin.rearrange("p (t d) -> p t d", d=D)
        for tk in range(ntt):
            nt = min(128, N - 128 * tk)
            dma("sp", xin3[0:nt, tk, :], src[t0 + 128 * tk:t0 + 128 * tk + nt, :], (), [K("xin")])
        for f in range(8):
            ps = next_ps()
            for tk in range(ntt):
                nt = min(128, N - 128 * tk)
                S.add("pe", lambda e, ps=ps, tk=tk, nt=nt, f=f, xin3=xin3: e.transpose(out=ps[:, 128 * tk:128 * tk + nt], in_=xin3[0:nt, tk, 128 * f:128 * f + 128],
                                                                                identity=ident[0:nt, 0:nt]),
                      [K("xin"), K("ident")], [K("ps", ps.name)])
            cp(x_f[:, f, 0:N], ps[:, 0:N], [K("ps", ps.name)], [K("x_f", f)], eng="act")
            if ntt != 4:
                cp(x_b[:, f, 0:N], ps[:, 0:N], [K("ps", ps.name)], [K("x_b", f)], eng="dve")

    def store_y(C, dst, t0, N):
        activate(C)
        yo3 = yout.rearrange("p (t d) -> p t d", d=D)
        ntt = (N + 127) // 128
        for tk in range(ntt):
            nt = min(128, N - 128 * tk)
            for hh in range(2):
                ps = next_ps()
                for ff in range(4):
                    f = 4 * hh + ff
                    S.add("pe", lambda e, ps=ps, tk=tk, nt=nt, f=f, ff=ff, xf=x_f: e.transpose(out=ps[0:nt, 128 * ff:128 * ff + 128], in_=xf[:, f, 128 * tk:128 * tk + nt],
                                                                                       identity=ident[:, :]),
                          [K("x_f", f), K("ident")], [K("ps", ps.name)])
                cp(yo3[0:nt, tk, 512 * hh:512 * hh + 512], ps[0:nt, 0:512], [K("ps", ps.name)], [K("yout")], eng="act")
            dma("sp", dst[t0 + 128 * tk:t0 + 128 * tk + nt, :], yo3[0:nt, tk, :], [K("yout")], (), output=True)

    def run_layer(l, ctxs, last):
        gens = []
        for (C, N, nseq, L, prompt) in ctxs:
            activate(C)
            gens.append(layer_pass(l, N, nseq, L, prompt, last and prompt))
        reqs = []
        for (C, *_), g in zip(ctxs, gens):
            activate(C)
            reqs.append(next(g, None))
        while any(r is not None for r in reqs):
            r0 = [r for r in reqs if r is not None][0]
            view = load_w(r0[1], r0[2], r0[3])
            new = []
            for (C, *_), g, r in zip(ctxs, gens, reqs):
                if r is None:
                    new.append(None)
                    continue
                activate(C)
                try:
                    new.append(g.send(view))
                except StopIteration:
                    new.append(None)
            reqs = new

    fold = dbg.get('fold', True)
    SCTX = (CS, NSS * DL, NSS, DL, False)
    passes = [(512 * c, [(CP, 512, 1, 512, True)] + ([SCTX] if (fold and c == 0) else [])) for c in range(4)]
    if not fold:
        passes.append((0, [SCTX]))
    if 'passes' in dbg:
        passes = [passes[i] for i in dbg['passes']]
    try:
        for pi, (t0, ctxs) in enumerate(passes):
            ps_nrot[0] = 6 if any(not c_[4] for c_ in ctxs) else 7
            if pi >= 1 and fold and dbg.get('xtemps', True) and len(tf_pool) == NT:
                for ap_, nm_, keys_ in tf_extra:
                    AL[(nm_,)] = keys_
                    tf_pool.append(_TV(ap_, nm_))
                tf_n[0] = len(tf_pool)
            for (C, N, nseq, L, prompt) in ctxs:
                load_x(C, xp if prompt else xs, t0 if prompt else 0, N)
            chk('xin')
            for l in range(dbg.get('layers', DEPTH)):
                if pi == 0 and l == 1:
                    prep_tables(1)
                run_layer(l, ctxs, last=(pi == dbg.get('last_pass', 3)))
            for (C, N, nseq, L, prompt) in ctxs:
                store_y(C, yp if prompt else ys, t0 if prompt else 0, N)
        activate(CP)

    except _Stop:
        activate(CP)
        ypv = yp.rearrange("(p a) d -> p (a d)", p=128)
        dma("sp", ypv[:, 0:4096], m_f[:, :, :].rearrange("p f n -> p (f n)"), [K("m_f", f) for f in range(8)], (), output=True)
        dma("sp", ypv[:, 4096:8192], x_f[:, :, :].rearrange("p f n -> p (f n)"), [K("x_f", f) for f in range(8)], (), output=True)
        dma("sp", ypv[:, 8192:12288], arena[:, 0:4096], [K("r_f")], (), output=True)
    S.emit(es)
    if dbg.get('verbose'):
        print('ops', len(S.all), 'est makespan us', S.makespan)
    es.close()
    return nc


_CACHE = {}


def kernel(**inputs):
    f32 = lambda a: np.ascontiguousarray(np.asarray(a, dtype=np.float32))
    inp = {k: f32(v) for k, v in inputs.items()}
    if "nc" not in _CACHE:
        _CACHE["nc"] = build_program()
    nc = _CACHE["nc"]
    shared = {}
    for k in ("w_in", "conv_a_w", "conv_a_b", "gate_x_w", "gate_x_b", "gate_a_w", "gate_a_b", "lru_lambda", "conv_b_w",
              "ssm_d", "glu_w", "glu_b", "proj_a", "proj_b", "proj_c", "w_out", "ln1_g", "ln1_b", "mlp_up", "mlp_down", "ln2_g", "ln2_b"):
        shared[k] = inp[k]
    for k in ("ssm_a_re", "ssm_a_im", "ssm_log_dt"):
        shared[k] = inp[k].reshape(DEPTH, 2048)
    for k in ("ssm_b_re", "ssm_b_im"):
        shared[k] = inp[k].reshape(DEPTH, 2048 * 16)
    for k in ("ssm_c_re", "ssm_c_im"):
        shared[k] = inp[k].reshape(DEPTH, 512 * 64)
    in_maps = []
    for c in range(NCORES):
        sl = slice(NSS * c, NSS * (c + 1))
        m = dict(shared)
        m["xp"] = np.ascontiguousarray(inp["x_prompt"][c])
        m["xs"] = np.ascontiguousarray(inp["x_sample"][sl].reshape(NSS * DL, D))
        m["st_conv"] = np.ascontiguousarray(inp["state_lru_conv"][:, sl])
        m["st_h"] = np.ascontiguousarray(inp["state_lru_h"][:, sl])
        m["st_sc"] = np.ascontiguousarray(inp["state_sconv"][:, sl])
        m["st_re"] = np.ascontiguousarray(inp["state_ssm_re"][:, sl].reshape(DEPTH, NSS, 2048))
        m["st_im"] = np.ascontiguousarray(inp["state_ssm_im"][:, sl].reshape(DEPTH, NSS, 2048))
        in_maps.append(m)
    res = run_bass_kernel_spmd(nc, in_maps, core_ids=list(range(NCORES)))
    R = res.results
    y_prompt = np.stack([R[c]["yp"] for c in range(NCORES)], 0)
    y_sample = np.concatenate([R[c]["ys"].reshape(NSS, DL, D) for c in range(NCORES)], 0)
    stk = lambda name, shp: np.stack([R[c][name].reshape(shp) for c in range(NCORES)], 1)
    cat = lambda name, shp: np.concatenate([R[c][name].reshape(shp) for c in range(NCORES)], 1)
    outs = (y_prompt, y_sample,
            stk("o_conv_p", (DEPTH, 3, 512)), stk("o_h_p", (DEPTH, 512)), stk("o_sc_p", (DEPTH, 2, 512)),
            stk("o_re_p", (DEPTH, G, 64)), stk("o_im_p", (DEPTH, G, 64)),
            cat("o_conv_s", (DEPTH, NSS, 3, 512)), cat("o_h_s", (DEPTH, NSS, 512)), cat("o_sc_s", (DEPTH, NSS, 2, 512)),
            cat("o_re_s", (DEPTH, NSS, G, 64)), cat("o_im_s", (DEPTH, NSS, G, 64)))
    return tuple(np.ascontiguousarray(o, dtype=np.float32) for o in outs)
```

```python
import math
from contextlib import ExitStack
import numpy as np
import concourse.bass as bass
import concourse.mybir as mybir
from concourse.bass_utils import run_bass_kernel_spmd

F32 = mybir.dt.float32
BF16 = mybir.dt.bfloat16
I32 = mybir.dt.int32
AF = mybir.ActivationFunctionType
ALU = mybir.AluOpType

NCORES = 8
D = 1024
DEPTH = 2
SEQ = 2048
NSS = 16
DL = 4
G = 32
TB = 128
ALPHA = (2 * DEPTH) ** 0.25
MAGIC = 12582912.0
TWO_PI = 2.0 * math.pi
S5ADD = "pool"
POOLX = "pool"
STRICT_WAR = True

ENG_ATTR = {"pe": "tensor", "act": "scalar", "dve": "vector", "pool": "gpsimd", "sp": "sync"}


class Op:
    __slots__ = ("eng", "fn", "deps", "odeps", "signal", "sigval", "dma", "dsem", "dval", "dur", "nbytes", "seq",
                 "succ", "npred", "t_end", "output")

    def __init__(self, eng, fn, dma):
        self.eng = eng
        self.fn = fn
        self.deps = []
        self.odeps = []
        self.signal = False
        self.sigval = 0
        self.dma = dma
        self.dsem = None
        self.dval = 0
        self.dur = 0.3
        self.nbytes = 0
        self.output = False


class Sched:
    def __init__(self, nc):
        self.nc = nc
        self.all = []
        self.last_w = {}
        self.readers = {}
        self.ndma = {"sp": 8, "pool": 6, "act": 2}
        self.reorder = True
        self.alias = {}

    def add(self, eng, fn, reads=(), writes=(), dma=False, output=False, dur=None, nbytes=0):
        op = Op(eng, fn, dma)
        op.seq = len(self.all)
        op.output = output
        op.nbytes = nbytes
        if dur is not None:
            op.dur = dur
        deps = {}
        al = self.alias
        reads = list(reads) + [r for k in reads for r in al.get(k, ())]
        writes = list(writes) + [r for k in writes for r in al.get(k, ())]
        writes = list(writes) + [k for k in reads if k[0] == "ps" and k not in writes]
        for k in reads:
            w = self.last_w.get(k)
            if w is not None:
                deps[id(w)] = (w, True)
        for k in writes:
            w = self.last_w.get(k)
            if w is not None:
                deps[id(w)] = (w, True)
            for r in self.readers.get(k, ()):
                if id(r) not in deps:
                    deps[id(r)] = (r, False)
        for d, strong in deps.values():
            if d is op:
                continue
            op.odeps.append(d)
            if (not d.dma) and d.eng == eng:
                if eng == "pe" or not (strong or STRICT_WAR):
                    continue
            op.deps.append(d)
        for k in reads:
            self.readers.setdefault(k, []).append(op)
        for k in writes:
            self.last_w[k] = op
            self.readers[k] = []
        self.all.append(op)
        return op

    def schedule(self):
        import heapq
        ops = self.all
        for op in ops:
            op.succ = []
            op.npred = 0
        for op in ops:
            seen = set()
            for d in op.odeps:
                if id(d) in seen:
                    continue
                seen.add(id(d))
                d.succ.append(op)
                op.npred += 1
        order = {e: [] for e in ENG_ATTR}
        if not self.reorder:
            for op in ops:
                order[op.eng].append(op)
            return order
        free = {e: 0.0 for e in ENG_ATTR}
        ready = {e: [] for e in ENG_ATTR}
        rtime = {}
        for op in ops:
            if op.npred == 0:
                heapq.heappush(ready[op.eng], (op.seq, op))
                rtime[id(op)] = 0.0
        dma_pipe = [0.0]
        nleft = len(ops)
        use_cp = getattr(self, "use_cp", False)
        if use_cp:
            for op in reversed(ops):
                d = (op.nbytes / 160e3 + 2.0) if op.dma else op.dur
                op.sigval = d + max([s_.sigval for s_ in op.succ], default=0.0)
        LOOK = getattr(self, 'look', 24)
        while nleft:
            best = None
            for e in ENG_ATTR:
                h = ready[e]
                if not h:
                    continue
                cand = heapq.nsmallest(LOOK, h)
                for seq, op in cand:
                    est = max(free[e], rtime[id(op)])
                    key = (est, -op.sigval, seq) if use_cp else (est, seq)
                    if best is None or key < best[0]:
                        best = (key, e, op)
            est, e, op = best[0][0], best[1], best[2]
            ready[e].remove((op.seq, op))
            heapq.heapify(ready[e])
            if op.dma:
                issue = 0.9 if e == "pool" else 0.15
                t0 = est + issue
                xfer = op.nbytes / getattr(self, 'dma_bw', 300e3)
                st = max(t0, dma_pipe[0])
                dma_pipe[0] = st + xfer
                op.t_end = st + xfer + 2.0
                free[e] = t0
            else:
                op.t_end = est + op.dur
                free[e] = op.t_end
            order[e].append(op)
            nleft -= 1
            for s_ in op.succ:
                s_.npred -= 1
                lat = 0.05 if (s_.eng == e and not op.dma) else 0.2
                rt = max(rtime.get(id(s_), 0.0), op.t_end + lat)
                rtime[id(s_)] = rt
                if s_.npred == 0:
                    heapq.heappush(ready[s_.eng], (s_.seq, s_))
        self.makespan = max(free.values())
        return order

    def emit(self, es):
        nc = self.nc
        order = self.schedule()
        for op in self.all:
            op.sigval = 0
        out_dmas = []
        for q, n in self.ndma.items():
            c = 0
            last = {}
            for op in order[q]:
                if not op.dma:
                    continue
                slot = c % n
                op.dsem = (q, slot)
                op.dval = 16 * (c // n + 1)
                prev = last.get(slot)
                if prev is not None:
                    op.deps.append(prev)
                last[slot] = op
                c += 1
                if op.output:
                    out_dmas.append(op)
        for e in ENG_ATTR:
            for op in order[e]:
                for d in op.deps:
                    if not d.dma:
                        d.signal = True
        fin = Op("sp", None, False)
        fin.deps = list(out_dmas)
        order["sp"].append(fin)
        sems = {e: es.enter_context(nc.semaphore("sem_" + e)) for e in ENG_ATTR}
        dsems = {}
        for q, n in self.ndma.items():
            for s in range(n):
                dsems[(q, s)] = es.enter_context(nc.semaphore("dsem_%s_%d" % (q, s)))
        for e in ENG_ATTR:
            c = 0
            for op in order[e]:
                if op.signal:
                    c += 1
                    op.sigval = c
        block = es.enter_context(nc.Block())

        def run(eng_handle, e):
            waited = {}
            for op in order[e]:
                need = {}
                for d in op.deps:
                    if d.dma:
                        key = ("d",) + d.dsem
                        sem = dsems[d.dsem]
                        val = d.dval
                    else:
                        key = ("e", d.eng)
                        sem = sems[d.eng]
                        val = d.sigval
                    if val > need.get(key, (None, 0))[1]:
                        need[key] = (sem, val)
                for key, (sem, val) in need.items():
                    if waited.get(key, 0) >= val:
                        continue
                    waited[key] = val
                    eng_handle.wait_ge(sem, val)
                if op.fn is None:
                    continue
                ins = op.fn(eng_handle)
                if op.dma:
                    ins.then_inc(dsems[op.dsem], 16)
                elif op.signal:
                    ins.then_inc(sems[e], 1)

        for e, attr in ENG_ATTR.items():
            getattr(block, attr)(lambda h, e=e: run(h, e))


class _Stop(Exception):
    pass


def build_program(dbg=None):
    dbg = dbg or {}

    def chk(name):
        if dbg.get('stop') == name:
            raise _Stop()
    nc = bass.Bass("TRN2", target_bir_lowering=False)
    es = ExitStack()
    S = Sched(nc)
    S.reorder = not dbg.get('noreorder', False)
    if 'look' in dbg:
        S.look = dbg['look']
    S.use_cp = dbg.get('use_cp', True)
    if 'dma_bw' in dbg:
        S.dma_bw = dbg['dma_bw']

    def din(name, shape):
        return nc.dram_tensor(name, list(shape), F32, kind="ExternalInput").ap()

    def dout(name, shape):
        return nc.dram_tensor(name, list(shape), F32, kind="ExternalOutput").ap()

    xp = din("xp", [SEQ, D])
    xs = din("xs", [NSS * DL, D])
    st_conv = din("st_conv", [DEPTH, NSS, 3, 512])
    st_h = din("st_h", [DEPTH, NSS, 512])
    st_sc = din("st_sc", [DEPTH, NSS, 2, 512])
    st_re = din("st_re", [DEPTH, NSS, 2048])
    st_im = din("st_im", [DEPTH, NSS, 2048])
    w_in = din("w_in", [DEPTH, D, 6144])
    conv_a_w = din("conv_a_w", [DEPTH, 4, 512])
    conv_a_b = din("conv_a_b", [DEPTH, 512])
    gate_x_w = din("gate_x_w", [DEPTH, 8, 64, 64])
    gate_x_b = din("gate_x_b", [DEPTH, 512])
    gate_a_w = din("gate_a_w", [DEPTH, 8, 64, 64])
    gate_a_b = din("gate_a_b", [DEPTH, 512])
    lru_lambda = din("lru_lambda", [DEPTH, 512])
    conv_b_w = din("conv_b_w", [DEPTH, 3, 512])
    ssm_a_re = din("ssm_a_re", [DEPTH, 2048])
    ssm_a_im = din("ssm_a_im", [DEPTH, 2048])
    ssm_log_dt = din("ssm_log_dt", [DEPTH, 2048])
    ssm_b_re = din("ssm_b_re", [DEPTH, 2048 * 16])
    ssm_b_im = din("ssm_b_im", [DEPTH, 2048 * 16])
    ssm_c_re = din("ssm_c_re", [DEPTH, 512 * 64])
    ssm_c_im = din("ssm_c_im", [DEPTH, 512 * 64])
    ssm_d = din("ssm_d", [DEPTH, 512])
    glu_w = din("glu_w", [DEPTH, 512, 512])
    glu_b = din("glu_b", [DEPTH, 512])
    proj_a = din("proj_a", [DEPTH, 512, D])
    proj_b = din("proj_b", [DEPTH, 512, D])
    proj_c = din("proj_c", [DEPTH, 512, D])
    w_out = din("w_out", [DEPTH, D, D])
    ln1_g = din("ln1_g", [DEPTH, D])
    ln1_b = din("ln1_b", [DEPTH, D])
    mlp_up = din("mlp_up", [DEPTH, D, 4096])
    mlp_down = din("mlp_down", [DEPTH, 4096, D])
    ln2_g = din("ln2_g", [DEPTH, D])
    ln2_b = din("ln2_b", [DEPTH, D])

    yp = dout("yp", [SEQ, D])
    ys = dout("ys", [NSS * DL, D])
    o_conv_p = dout("o_conv_p", [DEPTH, 3, 512])
    o_h_p = dout("o_h_p", [DEPTH, 512])
    o_sc_p = dout("o_sc_p", [DEPTH, 2, 512])
    o_re_p = dout("o_re_p", [DEPTH, 2048])
    o_im_p = dout("o_im_p", [DEPTH, 2048])
    o_conv_s = dout("o_conv_s", [DEPTH, NSS, 3, 512])
    o_h_s = dout("o_h_s", [DEPTH, NSS, 512])
    o_sc_s = dout("o_sc_s", [DEPTH, NSS, 2, 512])
    o_re_s = dout("o_re_s", [DEPTH, NSS, 2048])
    o_im_s = dout("o_im_s", [DEPTH, NSS, 2048])

    es.enter_context(nc.allow_non_contiguous_dma(reason="small transposing loads/stores of per-feature vectors and states"))

    def sb(name, shape, dt=F32):
        return es.enter_context(nc.sbuf_tensor(name, list(shape), dt))

    NR = 7
    ring = [sb("ring%d" % i, [128, 4096], BF16) for i in range(NR)]
    ring_i = [0]
    x_f = sb("x_f", [128, 8, 512])
    x_b = sb("x_b", [128, 8, 512], BF16)
    VEC = {}
    nv = [0]

    def vslot(name, n):
        VEC[name] = nv[0]
        nv[0] += n

    for k in range(4):
        vslot("caw%d" % k, 4)
    vslot("cab", 4)
    vslot("gxb", 4)
    vslot("gab", 4)
    vslot("lam", 4)
    for k in range(3):
        vslot("cbw%d" % k, 4)
    vslot("ssd", 4)
    vslot("glb", 4)
    vslot("l1g", 8)
    vslot("l1b", 8)
    vslot("l2g", 8)
    vslot("l2b", 8)
    vslot("lc", 4)
    vslot("tmp", 4)
    NV = nv[0]
    vecs = [sb("vecs%d" % l, [128, NV]) for l in range(DEPTH)]
    gxw = [sb("gxw%d" % l, [128, 4, 128], BF16) for l in range(DEPTH)]
    gaw = [sb("gaw%d" % l, [128, 4, 128], BF16) for l in range(DEPTH)]
    ident = sb("ident", [128, 128])
    ones_b = sb("ones_b", [128, 128], BF16)
    cst = sb("cst", [128, 4])
    s5c = [sb("s5c%d" % l, [128, 12, 16]) for l in range(DEPTH)]
    C_R, C_AR, C_AI, C_NAI, C_CR, C_CI, C_NCI = 0, 1, 2, 3, 4, 5, 6
    NTB = 4
    tabbuf = [sb("tabbuf%d" % i, [128, 512]) for i in range(NTB)]
    tabs = nc.dram_tensor("tabs_scratch", [DEPTH, 16, 2, 128, 512], F32).ap()
    NBC = 4
    bcbuf = [sb("bcbuf%d" % i, [128, 5, 128], BF16) for i in range(NBC)]
    bc_scr = nc.dram_tensor("bc_scratch", [DEPTH, 16, 128, 4, 128], BF16).ap()
    us_f = sb("us_f", [128, 2048])
    us_b = sb("us_b", [128, 2048], BF16)
    convA_st = [sb("convA_st%d" % l, [128, 4, 3]) for l in range(DEPTH)]
    convB_st = [sb("convB_st%d" % l, [128, 4, 2]) for l in range(DEPTH)]
    lru_h = [sb("lru_h%d" % l, [128, 4]) for l in range(DEPTH)]
    ssm_h = [sb("ssm_h%d" % l, [128, 16, 2]) for l in range(DEPTH)]
    s_h0 = sb("s_h0", [128, 4, NSS])
    s_re0 = sb("s_re0", [128, 16, NSS])
    s_im0 = sb("s_im0", [128, 16, NSS])
    s_sc0 = sb("s_sc0", [128, 4, NSS, 2])
    sst = sb("sst", [128, 8, 128])
    stg_o = sb("stg_o", [128, 8, 128])
    out_rows = sst
    arena = sb("arena", [128, 6144])
    v_b = sb("v_b", [128, 4, 512], BF16)
    m_f = sb("m_f", [128, 8, 512])
    NT = 8
    tf = [sb("tf%d" % i, [128, 512]) for i in range(NT)]
    NTBF = 5
    tb_ = [sb("tb%d" % i, [128, 512], BF16) for i in range(NTBF)]
    tf_i = [0]
    tb_i = [0]
    small = sb("small", [128, 64])
    psum = [es.enter_context(nc.psum_tensor("ps%d" % i, [128, 512], F32)) for i in range(8)]
    ps_i = [0]
    ps_nrot = [6]

    def next_ps():
        p = psum[ps_i[0] % ps_nrot[0]]
        ps_i[0] += 1
        return p

    class _TV:
        def __init__(self, ap, name):
            self.ap = ap
            self.name = name

        def __getitem__(self, key):
            return self.ap[key]

    tf_pool = list(tf)
    tf_n = [NT]

    def tmpf():
        t = tf_pool[tf_i[0] % tf_n[0]]
        tf_i[0] += 1
        return t

    def tmpb():
        t = tb_[tb_i[0] % NTBF]
        tb_i[0] += 1
        return t

    upad = arena[:, 0:2080]
    xc_f = arena[:, 2080:4128]
    xc_b = arena[:, 4128:5152].bitcast(BF16)
    r_f = arena[:, 0:4096]
    r_b = arena[:, 4096:6144].bitcast(BF16)
    m_b = arena[:, 4096:6144].bitcast(BF16).rearrange("p (f n) -> p f n", f=8)
    xin = arena[:, 0:4096]
    zc_t = sb("zc_t", [128, 4, 512], BF16)
    hid_v = m_f[:, :, :].rearrange("p f n -> p (f n)").bitcast(BF16).rearrange("p (j n) -> p j n", j=16)
    yout = arena[:, 0:4096]

    NS_ = NSS * DL
    x_f_s = sb("x_f_s", [128, 8, NS_])
    x_b_s = sb("x_b_s", [128, 8, NS_], BF16)
    arena_s = sb("arena_s", [128, 1024])
    v_b_s = sb("v_b_s", [128, 4, NS_], BF16)
    m_f_s = sb("m_f_s", [128, 8, NS_])
    m_b_s = sb("m_b_s", [128, 8, NS_], BF16)
    zc_t_s = sb("zc_t_s", [128, 4, NS_], BF16)
    us_f_s = sb("us_f_s", [128, 4 * NS_])
    us_b_s = sb("us_b_s", [128, 4 * NS_], BF16)

    class Ctx:
        pass

    CP = Ctx()
    CP.tag = None
    CP.t = dict(x_f=x_f, x_b=x_b, upad=upad, xc_f=xc_f, xc_b=xc_b, r_f=r_f, r_b=r_b, xin=xin, yout=yout, v_b=v_b, m_f=m_f, m_b=m_b,
                zc_t=zc_t, hid_v=hid_v, us_f=us_f, us_b=us_b)
    CS = Ctx()
    CS.tag = "s"
    CS.t = dict(x_f=x_f_s, x_b=x_b_s, upad=arena_s[:, 0:448], xc_f=arena_s[:, 448:704], xc_b=arena_s[:, 704:832].bitcast(BF16),
                r_f=arena_s[:, 0:512], r_b=arena_s[:, 512:768].bitcast(BF16), xin=arena_s[:, 0:1024], yout=arena_s[:, 0:1024],
                v_b=v_b_s, m_f=m_f_s, m_b=m_b_s, zc_t=zc_t_s,
                hid_v=m_f_s[:, :, :].rearrange("p f n -> p (f n)").bitcast(BF16).rearrange("p (j n) -> p j n", j=16),
                us_f=us_f_s, us_b=us_b_s)
    tf_extra = [(x_f_s[:, :, :].rearrange("p f n -> p (f n)"), "tfx0", [("s", "x_f", f_) for f_ in range(8)]),
                (m_f_s[:, :, :].rearrange("p f n -> p (f n)"), "tfx1", [("s", "m_f", f_) for f_ in range(8)] + [("s", "hid")]),
                (arena_s[:, 0:512], "tfx2", [("ars", 0)]),
                (arena_s[:, 512:1024], "tfx3", [("ars", 1)])]
    cur_tag = [None]
    CTXKEYS = {"x_b", "x_f", "upad", "xc_f", "xc_b", "r_f", "r_b", "v_b", "m_f", "m_b", "zc", "hid", "us_f", "us_b", "xin", "yout"}

    def activate(C):
        nonlocal x_f, x_b, upad, xc_f, xc_b, r_f, r_b, xin, yout, v_b, m_f, m_b, zc_t, hid_v, us_f, us_b
        t = C.t
        x_f, x_b, upad, xc_f, xc_b, r_f, r_b = t["x_f"], t["x_b"], t["upad"], t["xc_f"], t["xc_b"], t["r_f"], t["r_b"]
        xin, yout, v_b, m_f, m_b, zc_t, hid_v, us_f, us_b = t["xin"], t["yout"], t["v_b"], t["m_f"], t["m_b"], t["zc_t"], t["hid_v"], t["us_f"], t["us_b"]
        cur_tag[0] = C.tag

    def K(*a):
        if cur_tag[0] is not None and a[0] in CTXKEYS:
            return (cur_tag[0],) + a
        return a

    def blocks(tag, lo, hi, bs=512):
        return [(tag, b) for b in range(lo // bs, (hi - 1) // bs + 1)]

    AL = S.alias
    AL[K("upad")] = blocks("ar", 0, 2080)
    AL[K("xin")] = blocks("ar", 0, 4096)
    AL[K("yout")] = blocks("ar", 0, 4096)
    AL[K("r_f")] = blocks("ar", 0, 4096)
    AL[K("r_b")] = blocks("ar", 4096, 6144)
    AL[K("BTs")] = blocks("ar", 0, 2048)
    AL[K("CTs")] = blocks("ar", 2048, 4096)
    AL[K("masks")] = blocks("ar", 4096, 5120)
    AL[K("cdup")] = blocks("ar", 5120, 6144)
    for a_ in range(4):
        AL[K("xc_f", a_)] = blocks("ar", 2080 + 512 * a_, 2080 + 512 * (a_ + 1))
        AL[K("xc_b", a_)] = blocks("ar", 4128 + 256 * a_, 4128 + 256 * (a_ + 1))
    for f_ in range(8):
        AL[K("m_f", f_)] = [("mf", f_)]
        AL[K("m_b", f_)] = blocks("ar", 4096 + 256 * f_, 4096 + 256 * (f_ + 1))
    AL[K("hid")] = [("mf", f_) for f_ in range(8)]
    AL[K("braw")] = [("mf", 0)]
    AL[K("bbar")] = [("mf", 1)]
    for j_ in range(4):
        AL[K("bpad", j_)] = [("mf", 2)]
    AL[K("s5t", 0)] = [("mf", 3)]
    AL[K("s5t", 1)] = [("mf", 3)]
    AL[K("gstage")] = [("mf", 4)]
    AL[K("io_i")] = [("mf", 5)]
    AL[K("tau_i")] = [("mf", 5), ("mf", 6)]
    AL[K("vstage")] = [("mf", 6)]
    AL[K("sstage")] = [("mf", 7)]
    AL[("s", "upad")] = blocks("ars", 0, 448)
    AL[("s", "xin")] = blocks("ars", 0, 1024)
    AL[("s", "yout")] = blocks("ars", 0, 1024)
    AL[("s", "r_f")] = blocks("ars", 0, 512)
    AL[("s", "r_b")] = blocks("ars", 512, 768)
    for a_ in range(4):
        AL[("s", "xc_f", a_)] = blocks("ars", 448 + 64 * a_, 448 + 64 * (a_ + 1))
        AL[("s", "xc_b", a_)] = blocks("ars", 704 + 32 * a_, 704 + 32 * (a_ + 1))
    for f_ in range(8):
        AL[("s", "m_f", f_)] = [("mfs", 0)]
    AL[("s", "hid")] = [("mfs", 0)]

    def act(out, in_, func, reads, writes, bias=None, scale=None):
        kw = {}
        if bias is not None:
            kw["bias"] = bias
        if scale is not None:
            kw["scale"] = scale
        return S.add("act", lambda e: e.activation(out=out, in_=in_, func=func, **kw), list(reads) + [("cst",)], writes, dur=0.22 + out.free_size() / 1400.0)

    def edur(eng, n, c):
        if eng == "pool":
            return 0.25 + n * 2.0 / 1000.0
        return 0.12 + n * c / 960.0

    def tt(out, in0, in1, op, reads, writes, eng="dve"):
        c = 1.0 if any(k[0] == "ps" for k in reads) else 2.0
        return S.add(eng, lambda e: e.tensor_tensor(out=out, in0=in0, in1=in1, op=op), reads, writes, dur=edur(eng, out.free_size(), c))

    def ts(out, in0, s1, s2, op0, op1, reads, writes, eng="dve"):
        if op1 is None:
            return S.add(eng, lambda e: e.tensor_scalar(out=out, in0=in0, scalar1=s1, scalar2=None, op0=op0), reads, writes, dur=edur(eng, out.free_size(), 1.0))
        return S.add(eng, lambda e: e.tensor_scalar(out=out, in0=in0, scalar1=s1, scalar2=s2, op0=op0, op1=op1), reads, writes, dur=edur(eng, out.free_size(), 1.0))

    def stt(out, in0, scalar, in1, op0, op1, reads, writes, eng="dve"):
        return S.add(eng, lambda e: e.scalar_tensor_tensor(out=out, in0=in0, scalar=scalar, in1=in1, op0=op0, op1=op1), reads, writes, dur=edur(eng, out.free_size(), 1.4))

    def cp(out, in_, reads, writes, eng="dve"):
        if eng == "act":
            return S.add(eng, lambda e: e.activation(out=out, in_=in_, func=AF.Copy), reads, writes, dur=0.22 + out.free_size() / 1400.0)
        return S.add(eng, lambda e: e.tensor_copy(out=out, in_=in_), reads, writes, dur=edur(eng, out.free_size(), 1.0))

    def mset(ap, val, writes, eng="dve"):
        return S.add(eng, lambda e: e.memset(ap, val), (), writes)

    def dma(q, out, in_, reads, writes, output=False):
        return S.add(q, lambda e: e.dma_start(out=out, in_=in_), reads, writes, dma=True, output=output, nbytes=4 * out.size())

    def mm(ps, n, pairs, reads):
        last = len(pairs) - 1
        for i, (l_, r_) in enumerate(pairs):
            S.add("pe", lambda e, l_=l_, r_=r_, i=i: e.matmul(ps[:, 0:n], lhsT=l_, rhs=r_, start=(i == 0), stop=(i == last)),
                  reads, [K("ps", ps.name)], dur=0.015 + n / 2150.0)

    def load_w(src3, kt, ncol):
        i = ring_i[0] % NR
        ring_i[0] += 1
        slot = ring[i]
        view = slot[:, 0:kt * ncol].rearrange("p (k c) -> p k c", k=kt)
        S.add("pool", lambda e: e.dma_start(out=view, in_=src3), (), [K("ring", i)], dma=True, nbytes=4 * 128 * kt * ncol)
        return view, K("ring", i)

    def w_rows(mat2d, r0, nrow, c0, ncol):
        return mat2d[r0:r0 + nrow, c0:c0 + ncol].rearrange("(k p) c -> p k c", p=128)

    mflat0 = m_f[:, :, :].rearrange("p f n -> p (f n)")
    io_i = mflat0[:, 2560:2688].bitcast(I32)
    S.add("pool", lambda e: e.iota(io_i[:], pattern=[[1, 128]], base=0, channel_multiplier=-1), (), [K("io_i")])
    S.add("dve", lambda e: e.tensor_single_scalar(out=ident[:], in_=io_i[:], scalar=0, op=ALU.is_equal), [K("io_i")], [K("ident")])
    tau_i = mflat0[:, 2944:3456].bitcast(I32)
    tau_f = sb("tau_f", [128, 512])
    S.add("pool", lambda e: e.iota(tau_i[:], pattern=[[1, 512]], base=1, channel_multiplier=0), (), [K("tau_i")])
    cp(tau_f[:], tau_i[:], [K("tau_i")], [K("tau_f")])
    mset(ones_b[:], 1.0, [K("ones_b")])
    mset(cst[:, 0:1], 1e-5, [K("cst")])
    mset(cst[:, 1:2], 1.0, [K("cst")])
    mset(cst[:, 2:3], 0.0, [K("cst")])
    mset(cst[:, 3:4], 4e-5, [K("cst")])
    EPS = cst[:, 0:1]
    ONE = cst[:, 1:2]
    EPS4 = cst[:, 3:4]
    for l in range(DEPTH):
        mset(convA_st[l][:], 0.0, [K("convA_st", l)])
        mset(convB_st[l][:], 0.0, [K("convB_st", l)])
        mset(lru_h[l][:], 0.0, [K("lru_h", l)])
        mset(ssm_h[l][:], 0.0, [K("ssm_h", l, i) for i in range(16)])

    def V(l, name, a):
        c = VEC[name] + a
        return vecs[l][:, c:c + 1]

    masks = arena[:, 4096:5120].rearrange("p (m c) -> p m c", m=8)
    mset(masks[:], 0.0, [K("masks")])
    for jj in range(4):
        for sg, val in ((0, 1.0), (1, -1.0)):
            j0 = 2 * jj
            mset(masks[0:64, 2 * jj + sg, 16 * j0:16 * j0 + 16], val, [K("masks")])
            mset(masks[64:128, 2 * jj + sg, 16 * (j0 + 1):16 * (j0 + 1) + 16], val, [K("masks")])

    mflat = m_f[:, :, :].rearrange("p f n -> p (f n)")
    BTs = arena[:, 0:2048].bitcast(BF16).rearrange("p (i r c) -> p i r c", i=16, r=2)
    CTs = arena[:, 2048:4096].bitcast(BF16).rearrange("p (i r c) -> p i r c", i=16, r=2)
    braw = mflat[:, 0:512].rearrange("p (r i c) -> p r i c", r=2, i=16)
    bbar = mflat[:, 512:1024].rearrange("p (r i c) -> p r i c", r=2, i=16)
    bpads = [mflat[:, 1024 + 128 * j:1152 + 128 * j] for j in range(4)]
    s5t = mflat[:, 1536:1728].rearrange("p (k i) -> p k i", k=12)
    gstage = mflat[:, 2048:2560].rearrange("p (a c) -> p a c", a=4)
    cdup = arena[:, 5120:6144].rearrange("p (r a c) -> p r a c", r=2, a=4)
    ang = arena[:, 0:2048].rearrange("p (i t) -> p i t", i=16)
    frc = arena[:, 2048:4096].rearrange("p (i t) -> p i t", i=16)
    for j in range(4):
        mset(bpads[j], 0.0, [K("bpad", j)])

    def sincos(ang_ap, sin_out, cos_out, tmp1, tmp2, key_in, key_s, key_c, k1, k2):
        ts(tmp1, ang_ap, 1.0 / TWO_PI, MAGIC, ALU.mult, ALU.add, [key_in], [k1])
        ts(tmp1, tmp1, -MAGIC, None, ALU.add, None, [k1], [k1])
        stt(tmp1, ang_ap, 1.0 / TWO_PI, tmp1, ALU.mult, ALU.subtract, [key_in, k1], [k1])
        act(sin_out, tmp1, AF.Sin, [k1], [key_s], scale=TWO_PI)
        act(tmp2, tmp1, AF.Sin, [k1, key_in], [k2], scale=math.pi)
        tt(tmp2, tmp2, tmp2, ALU.mult, [k2], [k2])
        ts(cos_out, tmp2, -2.0, 1.0, ALU.mult, ALU.add, [k2], [key_c])

    def prep_tables(l):
        kc = K("s5c", l)
        for i in range(16):
            angp, frcp, sinp, cosp = tmpf(), tmpf(), tmpf(), tmpf()
            ts(angp[:, :], tau_f[:, :], s5c[l][:, 7, i:i + 1], None, ALU.mult, None, [K("tau_f"), kc], [K(angp.name)])
            sincos(angp[:, :], sinp[:, :], cosp[:, :], frcp[:, :], angp[:, :], K(angp.name), K(sinp.name), K(cosp.name), K(frcp.name), K(angp.name))
            dma("sp", tabs[l, i, 0], cosp[:, :], [K(cosp.name)], [K("tabs", l)])
            dma("sp", tabs[l, i, 1], sinp[:, :], [K(sinp.name)], [K("tabs", l)])

    for l in range(DEPTH):
        kv = K("vecs", l)
        vec_src = [("caw0", conv_a_w[l, 0]), ("caw1", conv_a_w[l, 1]), ("caw2", conv_a_w[l, 2]), ("caw3", conv_a_w[l, 3]),
                   ("cab", conv_a_b[l]), ("gxb", gate_x_b[l]), ("gab", gate_a_b[l]), ("lam", lru_lambda[l]),
                   ("cbw0", conv_b_w[l, 0]), ("cbw1", conv_b_w[l, 1]), ("cbw2", conv_b_w[l, 2]),
                   ("ssd", ssm_d[l]), ("glb", glu_b[l]), ("l1g", ln1_g[l]), ("l1b", ln1_b[l]), ("l2g", ln2_g[l]), ("l2b", ln2_b[l])]
        vstage = m_f[:, :, :].rearrange("p f n -> p (f n)")[:, 3456:3584]
        sstage = m_f[:, :, :].rearrange("p f n -> p (f n)")[:, 3584:3712]
        nvr = 0
        for name, src in vec_src:
            n = src.shape[0] // 128
            c0 = VEC[name]
            dma("sp", vstage[c0:c0 + n, :], src.rearrange("(a p) -> a p", p=128), (), [K("vstage")])
            nvr = max(nvr, c0 + n)
        psv = next_ps()
        S.add("pe", lambda e, psv=psv, nvr=nvr, vstage=vstage: e.transpose(out=psv[:, 0:nvr], in_=vstage[0:nvr, :], identity=ident[0:nvr, 0:nvr]),
              [K("vstage"), K("ident")], [K("ps", psv.name)])
        cp(vecs[l][:, 0:nvr], psv[:, 0:nvr], [K("ps", psv.name)], [kv], eng="act")
        lam = vecs[l][:, VEC["lam"]:VEC["lam"] + 4]
        tmpv = vecs[l][:, VEC["tmp"]:VEC["tmp"] + 4]
        lcv = vecs[l][:, VEC["lc"]:VEC["lc"] + 4]
        act(tmpv, lam, AF.Exp, [kv], [kv], scale=-1.0)
        act(tmpv, tmpv, AF.Ln, [kv], [kv], bias=ONE)
        ts(lcv, tmpv, -8.0, None, ALU.mult, None, [kv], [kv])
        for (gw_src, gw_dst, nm) in ((gate_x_w, gxw, "gxw"), (gate_a_w, gaw, "gaw")):
            mset(gstage[:], 0.0, [K("gstage")])
            for a in range(4):
                dma("sp", gstage[0:64, a, 0:64], gw_src[l, 2 * a], (), [K("gstage")])
                dma("sp", gstage[64:128, a, 64:128], gw_src[l, 2 * a + 1], (), [K("gstage")])
            cp(gw_dst[l][:], gstage[:], [K("gstage")], [K(nm, l)])
        ks5 = K("s5t", l)
        kc = K("s5c", l)
        are, aim, ldt = s5t[:, 0, :], s5t[:, 1, :], s5t[:, 2, :]
        for j_, src_ in enumerate((ssm_a_re, ssm_a_im, ssm_log_dt)):
            dma("sp", sstage[16 * j_:16 * j_ + 16, :], src_[l].rearrange("(i q) -> i q", q=128), (), [K("sstage")])
        pss = next_ps()
        S.add("pe", lambda e, pss=pss, sstage=sstage: e.transpose(out=pss[:, 0:48], in_=sstage[0:48, :], identity=ident[0:48, 0:48]),
              [K("sstage"), K("ident")], [K("ps", pss.name)])
        cp(s5t[:, 0:3, :], pss[:, 0:48].rearrange("p (k i) -> p k i", k=3), [K("ps", pss.name)], [ks5], eng="act")
        step = s5t[:, 3, :]
        act(step, ldt, AF.Exp, [ks5], [ks5])
        sar, sai = s5t[:, 4, :], s5t[:, 5, :]
        tt(sar, step, are, ALU.mult, [ks5], [ks5])
        tt(sai, step, aim, ALU.mult, [ks5], [ks5])
        rr = s5c[l][:, C_R, :]
        act(rr, sar, AF.Exp, [ks5], [kc])
        sn, cs = s5t[:, 6, :], s5t[:, 7, :]
        sincos(sai, sn, cs, s5t[:, 8, :], s5t[:, 9, :], ks5, ks5, ks5, ks5, ks5)
        ar, ai, nai = s5c[l][:, C_AR, :], s5c[l][:, C_AI, :], s5c[l][:, C_NAI, :]
        tt(ar, rr, cs, ALU.mult, [ks5, kc], [kc])
        tt(ai, rr, sn, ALU.mult, [ks5, kc], [kc])
        ts(nai, ai, -1.0, None, ALU.mult, None, [kc], [kc])
        den, t8, t9, t10 = s5t[:, 8, :], s5t[:, 9, :], s5t[:, 10, :], s5t[:, 11, :]
        tt(den, are, are, ALU.mult, [ks5], [ks5])
        tt(t9, aim, aim, ALU.mult, [ks5], [ks5])
        tt(den, den, t9, ALU.add, [ks5], [ks5])
        S.add("dve", lambda e, den=den: e.reciprocal(out=den, in_=den), [ks5], [ks5])
        nr = s5t[:, 6, :]
        ts(nr, ar, -1.0, None, ALU.add, None, [kc, ks5], [ks5])
        tt(t9, nr, are, ALU.mult, [ks5], [ks5])
        tt(t10, ai, aim, ALU.mult, [ks5, kc], [ks5])
        tt(t9, t9, t10, ALU.add, [ks5], [ks5])
        tt(s5c[l][:, C_CR, :], t9, den, ALU.mult, [ks5], [kc])
        tt(t9, ai, are, ALU.mult, [ks5, kc], [ks5])
        tt(t10, nr, aim, ALU.mult, [ks5], [ks5])
        tt(t9, t9, t10, ALU.subtract, [ks5], [ks5])
        tt(s5c[l][:, C_CI, :], t9, den, ALU.mult, [ks5], [kc])
        ts(s5c[l][:, C_NCI, :], s5c[l][:, C_CI, :], -1.0, None, ALU.mult, None, [kc], [kc])
        cp(s5c[l][:, 7, :], sai, [ks5], [kc])
        kb = K("braw")
        dma("sp", braw[:, 0, :, :], ssm_b_re[l].rearrange("(i q c) -> q i c", q=128, c=16), (), [kb])
        dma("sp", braw[:, 1, :, :], ssm_b_im[l].rearrange("(i q c) -> q i c", q=128, c=16), (), [kb])
        crb = s5c[l][:, C_CR, :].unsqueeze(2).to_broadcast([128, 16, 16])
        cib = s5c[l][:, C_CI, :].unsqueeze(2).to_broadcast([128, 16, 16])
        kbb = K("bbar")
        t3 = tf[0][:, 0:256].rearrange("p (i c) -> p i c", c=16)
        tt(bbar[:, 0], braw[:, 0], crb, ALU.mult, [kb, kc], [kbb])
        tt(t3, braw[:, 1], cib, ALU.mult, [kb, kc], [K(tf[0].name)])
        tt(bbar[:, 0], bbar[:, 0], t3, ALU.subtract, [kbb, K(tf[0].name)], [kbb])
        tt(bbar[:, 1], braw[:, 1], crb, ALU.mult, [kb, kc], [kbb])
        tt(t3, braw[:, 0], cib, ALU.mult, [kb, kc], [K(tf[0].name)])
        tt(bbar[:, 1], bbar[:, 1], t3, ALU.add, [kbb, K(tf[0].name)], [kbb])
        for i in range(16):
            j0 = (2 * i) % 8
            for ri in range(2):
                bpad = bpads[j0 // 2]
                kbp = K("bpad", j0 // 2)
                cp(bpad[0:64, 16 * j0:16 * j0 + 16], bbar[0:64, ri, i, :], [kbb], [kbp])
                cp(bpad[64:128, 16 * (j0 + 1):16 * (j0 + 1) + 16], bbar[64:128, ri, i, :], [kbb], [kbp])
                ps = next_ps()
                S.add("pe", lambda e, ps=ps, bpad=bpad: e.transpose(out=ps[:, 0:128], in_=bpad, identity=ident[:]),
                      [kbp, K("ident")], [K("ps", ps.name)])
                cp(BTs[:, i, ri, :], ps[:, 0:128], [K("ps", ps.name)], [K("BTs")], eng="act")
        kcd = K("cdup")
        for ri, csrc in ((0, ssm_c_re), (1, ssm_c_im)):
            src = csrc[l].rearrange("(a r p) -> r a p", r=128, p=64)
            dma("sp", cdup[:, ri, :, 0:64], src, (), [kcd])
            dma("sp", cdup[:, ri, :, 64:128], src, (), [kcd])
        for ri in range(2):
            for a in range(4):
                ps = next_ps()
                S.add("pe", lambda e, ps=ps, ri=ri, a=a: e.transpose(out=ps[:, 0:128], in_=cdup[:, ri, a, :], identity=ident[:]),
                      [kcd, K("ident")], [K("ps", ps.name)])
                for i in range(4 * a, 4 * a + 4):
                    jj = ((2 * i) % 8) // 2
                    tt(CTs[:, i, ri, :], ps[:, 0:128], masks[:, 2 * jj + ri, :], ALU.mult,
                       [K("ps", ps.name), K("masks")], [K("CTs")])
        bcv = bc_scr[l].rearrange("i p r c -> p i r c")
        dma("sp", bcv[:, :, 0:2, :], BTs, [K("BTs")], [K("bc_scr", l)])
        dma("sp", bcv[:, :, 2:4, :], CTs, [K("CTs")], [K("bc_scr", l)])

    prep_tables(0)

    def layer_norm(l, N, gname, bname, eps_ap=None):
        eps_ap = EPS if eps_ap is None else eps_ap
        r3 = r_f.rearrange("p (f n) -> p f n", f=8)
        rb3 = r_b.rearrange("p (f n) -> p f n", f=8)
        kr = K("r_f")
        sq_b = hid_v
        for f in range(8):
            act(rb3[:, f, 0:N], r3[:, f, 0:N], AF.Copy, [kr], [K("r_b")])
            tt(sq_b[:, f, 0:N], r3[:, f, 0:N], r3[:, f, 0:N], ALU.mult, [kr], [K("hid")], eng=POOLX)
        ps_s = next_ps()
        mm(ps_s, N, [(ones_b[:], rb3[:, f, 0:N]) for f in range(8)], [K("ones_b"), K("r_b")])
        ps_q = next_ps()
        mm(ps_q, N, [(ones_b[:], sq_b[:, f, 0:N]) for f in range(8)], [K("ones_b"), K("hid")])
        mean = tmpf()
        act(mean[:, 0:N], ps_s[:, 0:N], AF.Copy, [K("ps", ps_s.name)], [K(mean.name)], scale=1.0 / D)
        msq = tmpf()
        tt(msq[:, 0:N], mean[:, 0:N], mean[:, 0:N], ALU.mult, [K(mean.name)], [K(msq.name)])
        var = tmpf()
        stt(var[:, 0:N], ps_q[:, 0:N], 1.0 / D, msq[:, 0:N], ALU.mult, ALU.subtract, [K("ps", ps_q.name), K(msq.name)], [K(var.name)])
        act(var[:, 0:N], var[:, 0:N], AF.Ln, [K(var.name)], [K(var.name)], bias=eps_ap)
        rstd = tmpf()
        act(rstd[:, 0:N], var[:, 0:N], AF.Exp, [K(var.name)], [K(rstd.name)], scale=-0.5)
        tpair = [tmpf(), tmpf()]
        for f in range(8):
            t = tpair[f % 2]
            tt(t[:, 0:N], r3[:, f, 0:N], mean[:, 0:N], ALU.subtract, [kr, K(mean.name)], [K(t.name)], eng=POOLX)
            tt(t[:, 0:N], t[:, 0:N], rstd[:, 0:N], ALU.mult, [K(t.name), K(rstd.name)], [K(t.name)])
            act(x_f[:, f, 0:N], t[:, 0:N], AF.Identity, [K(t.name), K("vecs", l)], [K("x_f", f)],
                bias=V(l, bname, f), scale=V(l, gname, f))
            act(x_b[:, f, 0:N], t[:, 0:N], AF.Identity, [K(t.name), K("vecs", l)], [K("x_b", f)],
                bias=V(l, bname, f), scale=V(l, gname, f))

    def merge_loads_g(l, br, proj_w):
        wp = yield ("load", w_rows(proj_w[l], 0, 512, 0, D), 4, D)
        wg = []
        for half in range(2):
            w_ = yield ("load", w_rows(w_in[l], 0, D, 3072 + 1024 * br + 512 * half, 512), 8, 512)
            wg.append(w_)
        return wp, wg

    def proj_merge_gen(l, N, br, vkey, loads):
        (wp, kwp), wgs = loads
        for half in range(2):
            wg, kwg = wgs[half]
            for ff in range(4):
                f = 4 * half + ff
                ps_g = next_ps()
                mm(ps_g, N, [(wg[:, kt, ff * 128:(ff + 1) * 128], x_b[:, kt, 0:N]) for kt in range(8)],
                   [kwg] + [K("x_b", kt) for kt in range(8)])
                g = tmpf()
                act(g[:, 0:N], ps_g[:, 0:N], AF.Tanh, [K("ps", ps_g.name)], [K(g.name)], scale=0.5)
                ps_o = next_ps()
                mm(ps_o, N, [(wp[:, kt, f * 128:(f + 1) * 128], v_b[:, kt, 0:N]) for kt in range(4)], [kwp, vkey])
                if br == 0:
                    stt(m_f[:, f, 0:N], g[:, 0:N], 1.0, ps_o[:, 0:N], ALU.add, ALU.mult, [K("ps", ps_o.name), K(g.name)], [K("m_f", f)])
                else:
                    stt(g[:, 0:N], g[:, 0:N], 1.0, ps_o[:, 0:N], ALU.add, ALU.mult, [K("ps", ps_o.name), K(g.name)], [K(g.name)])
                    if br == 1:
                        tt(m_f[:, f, 0:N], m_f[:, f, 0:N], g[:, 0:N], ALU.add, [K("m_f", f), K(g.name)], [K("m_f", f)], eng=S5ADD)
                    else:
                        tt(m_b[:, f, 0:N], m_f[:, f, 0:N], g[:, 0:N], ALU.add, [K("m_f", f), K(g.name)], [K("m_b", f)], eng=S5ADD)
            yield

    def layer_pass(l, N, nseq, L, prompt, last):
        kxb = [K("x_b", kt) for kt in range(8)]
        kv = K("vecs", l)
        sample = not prompt
        deferred = []

        def seqv(ap2d):
            return ap2d.rearrange("p (s t) -> p s t", t=L)

        PW = L + 3
        upA = upad[:, 0:4 * nseq * PW].rearrange("p (a s w) -> p a s w", a=4, s=nseq)
        xc3 = xc_f.rearrange("p (a n) -> p a n", a=4)
        xcb3 = xc_b.rearrange("p (a n) -> p a n", a=4)
        kup = K("upad")
        if sample:
            ksst = K("sst")
            srcc = st_conv[l].rearrange("s k (a p) -> (s k a) p", p=128)
            dma("sp", sst[0:120, 0, :], srcc[0:120], (), [ksst])
            dma("sp", sst[0:72, 1, :], srcc[120:192], (), [ksst])
            dma("sp", sst[0:64, 2, :], st_h[l].rearrange("s (a p) -> (s a) p", p=128), (), [ksst])
            dma("sp", sst[:, 3, :], st_sc[l].rearrange("s k (a p) -> (s k a) p", p=128), (), [ksst])
            srcr = st_re[l].rearrange("s (i q) -> (s i) q", q=128)
            srci = st_im[l].rearrange("s (i q) -> (s i) q", q=128)
            for hh in range(2):
                dma("sp", sst[:, 4 + hh, :], srcr[128 * hh:128 * hh + 128], (), [ksst])
                dma("sp", sst[:, 6 + hh, :], srci[128 * hh:128 * hh + 128], (), [ksst])
            nrows = [120, 72, 64, 128, 128, 128, 128, 128]
            for j in range(8):
                nr = nrows[j]
                ps = next_ps()
                S.add("pe", lambda e, ps=ps, j=j, nr=nr: e.transpose(out=ps[:, 0:nr], in_=sst[0:nr, j, :], identity=ident[0:nr, 0:nr]),
                      [ksst, K("ident")], [K("ps", ps.name)])
                kps = K("ps", ps.name)
                if j == 0:
                    cp(upA[:, :, 0:10, 0:3].rearrange("p a s k -> p s k a"), ps[:, 0:120].rearrange("p (s k a) -> p s k a", k=3, a=4),
                       [kps], [kup, K("r_f"), K("r_b")])
                elif j == 1:
                    cp(upA[:, :, 10:16, 0:3].rearrange("p a s k -> p s k a"), ps[:, 0:72].rearrange("p (s k a) -> p s k a", k=3, a=4),
                       [kps], [kup, K("r_f"), K("r_b")])
                elif j == 2:
                    cp(s_h0[:, :, :].rearrange("p a s -> p s a"), ps[:, 0:64].rearrange("p (s a) -> p s a", a=4), [kps], [K("s_h0")])
                elif j == 3:
                    cp(s_sc0[:, :, :, :].rearrange("p a s k -> p s k a"), ps[:, 0:128].rearrange("p (s k a) -> p s k a", k=2, a=4), [kps], [K("s_sc0")])
                elif j in (4, 5):
                    hh = j - 4
                    cp(s_re0[:, :, 8 * hh:8 * hh + 8].rearrange("q i s -> q s i"), ps[:, 0:128].rearrange("q (s i) -> q s i", i=16), [kps], [K("s_re0")])
                else:
                    hh = j - 6
                    cp(s_im0[:, :, 8 * hh:8 * hh + 8].rearrange("q i s -> q s i"), ps[:, 0:128].rearrange("q (s i) -> q s i", i=16), [kps], [K("s_im0")])
        PWB = L + 2
        upB = upad[:, 0:4 * nseq * PWB].rearrange("p (a s w) -> p a s w", a=4, s=nseq)

        us3 = us_f.rearrange("p (a n) -> p a n", a=4)
        usb3 = us_b.rearrange("p (a n) -> p a n", a=4)
        wus, kwus = yield ("load", w_rows(w_in[l], 0, D, 2560, 512), 8, 512)
        for a in range(4):
            ps = next_ps()
            mm(ps, N, [(wus[:, kt, a * 128:(a + 1) * 128], x_b[:, kt, 0:N]) for kt in range(8)], [kwus] + kxb)
            cp(us3[:, a, 0:N], ps[:, 0:N], [K("ps", ps.name)], [K("us_f", a)], eng="act")
            cp(usb3[:, a, 0:N], ps[:, 0:N], [K("ps", ps.name)], [K("us_b", a)], eng="act")
        kc = K("s5c", l)
        ps_y = psum[7] if prompt else psum[6]

        def s5_B(i):
            a = i // 4
            jb = i % NBC
            bc = bcbuf[jb]
            dma("sp", bc[:, 0:4, :], bc_scr[l, i], [K("bc_scr", l)], [K("bcbuf", jb)])
            act(bc[:, 4, :], bc[:, 2, :], AF.Copy, [K("bcbuf", jb)], [K("bcbuf", jb)], scale=-1.0)
            ps_r = next_ps()
            mm(ps_r, N, [(bc[:, 0, :], usb3[:, a, 0:N])], [K("bcbuf", jb), K("us_b", a)])
            ps_m = next_ps()
            mm(ps_m, N, [(bc[:, 1, :], usb3[:, a, 0:N])], [K("bcbuf", jb), K("us_b", a)])
            return ps_r, ps_m

        def s5_body(i, ps_r, ps_m):
            kpr, kpm = K("ps", ps_r.name), K("ps", ps_m.name)
            if prompt:
                Gr = next_ps()
                kGr = K("ps", Gr.name)
                Gi = tmpf()
                kGi = K(Gi.name)
                jc, js = (2 * i) % NTB, (2 * i + 1) % NTB
                tC, tS = tabbuf[jc], tabbuf[js]
                kct, kst = K("tabbuf", jc), K("tabbuf", js)
                dma("sp", tC[:, 0:N], tabs[l, i, 0][:, 0:N], [K("tabs", l)], [kct])
                dma("sp", tS[:, 0:N], tabs[l, i, 1][:, 0:N], [K("tabs", l)], [kst])
                cosb, sinb = tC[:, 0:N], tS[:, 0:N]
                bm = tmpf()
                cp(bm[:, 0:N], ps_m[:, 0:N], [kpm], [K(bm.name)], eng="act")
                t1, t2 = tmpf(), tmpf()
                tt(t1[:, 0:N], ps_r[:, 0:N], cosb, ALU.mult, [kpr, kct], [K(t1.name)])
                tt(t2[:, 0:N], bm[:, 0:N], sinb, ALU.mult, [K(bm.name), kst], [K(t2.name)], eng=S5ADD)
                tt(t1[:, 0:N], t1[:, 0:N], t2[:, 0:N], ALU.add, [K(t1.name), K(t2.name)], [K(t1.name)], eng=S5ADD)
                t3_, t4 = tmpf(), tmpf()
                tt(t3_[:, 0:N], ps_m[:, 0:N], cosb, ALU.mult, [kpm, kct], [K(t3_.name)])
                tt(t4[:, 0:N], ps_r[:, 0:N], sinb, ALU.mult, [kpr, kst], [K(t4.name)])
                tt(t3_[:, 0:N], t3_[:, 0:N], t4[:, 0:N], ALU.subtract, [K(t3_.name), K(t4.name)], [K(t3_.name)], eng=S5ADD)
                rbc = s5c[l][:, C_R, i:i + 1].to_broadcast([128, N])
                ksh = K("ssm_h", l, i)
                cN = tC[:, N - 1:N]
                sN = tS[:, N - 1:N]
                S.add("dve", lambda e, Gr=Gr, t1=t1, i=i, rbc=rbc: e.tensor_tensor_scan(
                    out=Gr[:, 0:N], data0=rbc, data1=t1[:, 0:N], initial=ssm_h[l][:, i, 0:1], op0=ALU.mult, op1=ALU.add),
                    [K(t1.name), kc, ksh], [kGr], dur=0.12 + 2 * N / 960.0)
                S.add("dve", lambda e, Gi=Gi, t3_=t3_, i=i, rbc=rbc: e.tensor_tensor_scan(
                    out=Gi[:, 0:N], data0=rbc, data1=t3_[:, 0:N], initial=ssm_h[l][:, i, 1:2], op0=ALU.mult, op1=ALU.add),
                    [K(t3_.name), kc, ksh], [kGi], dur=0.12 + 2 * N / 960.0)
                sm = small[:, 0:2]
                ts(sm[:, 0:1], Gi[:, N - 1:N], sN, None, ALU.mult, None, [kGi, kst], [K("small")])
                ts(sm[:, 1:2], Gr[:, N - 1:N], sN, None, ALU.mult, None, [kGr, kst], [K("small")])
                stt(ssm_h[l][:, i, 0:1], Gr[:, N - 1:N], cN, sm[:, 0:1], ALU.mult, ALU.subtract, [kGr, kct, K("small")], [ksh])
                stt(ssm_h[l][:, i, 1:2], Gi[:, N - 1:N], cN, sm[:, 1:2], ALU.mult, ALU.add, [kGi, kct, K("small")], [ksh])
                u1, u2, u3, u4 = tmpb(), tmpb(), tmpb(), tmpb()
                tt(u1[:, 0:N], Gr[:, 0:N], cosb, ALU.mult, [kGr, kct], [K(u1.name)])
                tt(u4[:, 0:N], Gr[:, 0:N], sinb, ALU.mult, [kGr, kst], [K(u4.name)])
                tt(u2[:, 0:N], Gi[:, 0:N], sinb, ALU.mult, [kGi, kst], [K(u2.name)], eng=S5ADD)
                tt(u3[:, 0:N], Gi[:, 0:N], cosb, ALU.mult, [kGi, kct], [K(u3.name)], eng=S5ADD)
                return (u1, u2, u3, u4)
            hb_r, hb_i = tmpb(), tmpb()
            if True:
                Gr, Gi = tmpf(), tmpf()
                kGr, kGi = K(Gr.name), K(Gi.name)
                arp = s5c[l][:, C_AR, i:i + 1]
                aip = s5c[l][:, C_AI, i:i + 1]
                naip = s5c[l][:, C_NAI, i:i + 1]
                u1 = small[:, 0:NSS]
                u2 = small[:, NSS:2 * NSS]
                for t in range(L):
                    pr = s_re0[:, i, :] if t == 0 else Gr[:, t - 1:N:L]
                    pi_ = s_im0[:, i, :] if t == 0 else Gi[:, t - 1:N:L]
                    stt(u1, pi_, naip, ps_r[:, t:N:L], ALU.mult, ALU.add, [kGi, K("s_im0"), kc, kpr], [K("small")])
                    stt(u2, pr, aip, ps_m[:, t:N:L], ALU.mult, ALU.add, [kGr, K("s_re0"), kc, kpm], [K("small")])
                    stt(Gr[:, t:N:L], pr, arp, u1, ALU.mult, ALU.add, [kGr, K("s_re0"), kc, K("small")], [kGr])
                    stt(Gi[:, t:N:L], pi_, arp, u2, ALU.mult, ALU.add, [kGi, K("s_im0"), kc, K("small")], [kGi])
                cp(stg_o[:, 4:6, i:128:16], Gr[:, L - 1:N:L].rearrange("p (h s) -> p h s", h=2), [kGr], [K("stg_o")])
                cp(stg_o[:, 6:8, i:128:16], Gi[:, L - 1:N:L].rearrange("p (h s) -> p h s", h=2), [kGi], [K("stg_o")])
                cp(hb_r[:, 0:N], Gr[:, 0:N], [kGr], [K(hb_r.name)], eng="act")
                cp(hb_i[:, 0:N], Gi[:, 0:N], [kGi], [K(hb_i.name)], eng="act")
            return hb_r, hb_i

        def s5_C(i, *hb):
            ii = i % 4
            jb = i % NBC
            bc = bcbuf[jb]
            if len(hb) == 4:
                terms = [(2, hb[0]), (4, hb[1]), (3, hb[2]), (3, hb[3])]
            else:
                terms = [(2, hb[0]), (3, hb[1])]
            nt_ = len(terms)
            for j_, (slot, hbt) in enumerate(terms):
                S.add("pe", lambda e, slot=slot, hbt=hbt, j_=j_: e.matmul(ps_y[:, 0:N], lhsT=bc[:, slot, :], rhs=hbt[:, 0:N],
                                                                     start=(ii == 0 and j_ == 0), stop=(ii == 3 and j_ == nt_ - 1)),
                      [K("bcbuf", jb), K(hbt.name)], [K("ps", ps_y.name)], dur=0.015 + N / 2150.0)

        def s5_epi(a):
            yt = tmpf()
            stt(yt[:, 0:N], us3[:, a, 0:N], V(l, "ssd", a), ps_y[:, 0:N], ALU.mult, ALU.add, [K("us_f", a), kv, K("ps", ps_y.name)], [K(yt.name)])
            act(zc_t[:, a, 0:N], yt[:, 0:N], AF.Gelu_apprx_tanh, [K(yt.name)], [K("zc", a)])


        lA0 = yield ("load", w_rows(w_in[l], 0, D, 0, 512), 8, 512)
        lA1 = yield ("load", w_rows(w_in[l], 0, D, 512, 512), 8, 512)
        loadsA = (lA0, lA1)
        post = {}

        def gen_units():
            (wxa, kwxa), (wya, kwya) = loadsA
            loads_mA = yield from merge_loads_g(l, 0, proj_a)
            for a in range(4):
                if prompt:
                    cp(upA[:, a, 0, 0:3], convA_st[l][:, a, :], [K("convA_st", l)], [kup], eng="act")
                ps = next_ps()
                mm(ps, N, [(wxa[:, kt, a * 128:(a + 1) * 128], x_b[:, kt, 0:N]) for kt in range(8)], [kwxa] + kxb)
                cp(upA[:, a, :, 3:3 + L], seqv(ps[:, 0:N]), [K("ps", ps.name)], [kup], eng="act")
                xo = seqv(xc3[:, a, 0:N])
                kxc = K("xc_f", a)
                act(xo, upA[:, a, :, 0:L], AF.Identity, [kup, kv], [kxc], bias=V(l, "cab", a), scale=V(l, "caw0", a))
                for k in range(1, 4):
                    stt(xo, upA[:, a, :, k:k + L], V(l, "caw%d" % k, a), xo, ALU.mult, ALU.add, [kup, kv, kxc], [kxc])
                if prompt:
                    cp(convA_st[l][:, a, :], upA[:, a, 0, L:L + 3], [kup], [K("convA_st", l)], eng="act")
                    if last:
                        dma("sp", o_conv_p[l].rearrange("k (a p) -> p a k", p=128)[:, a], upA[:, a, 0, L:L + 3], [kup], (), output=True)
                else:
                    cp(stg_o[:, 0, 0:120].rearrange("p (s k a) -> p s k a", k=3, a=4)[:, :, :, a], upA[:, a, 0:10, L:L + 3], [kup], [K("stg_o")])
                    cp(stg_o[:, 1, 0:72].rearrange("p (s k a) -> p s k a", k=3, a=4)[:, :, :, a], upA[:, a, 10:16, L:L + 3], [kup], [K("stg_o")])
                cp(xcb3[:, a, 0:N], xc3[:, a, 0:N], [kxc], [K("xc_b", a)], eng="act")
                ps_gx = next_ps()
                mm(ps_gx, N, [(gxw[l][:, a, :], xcb3[:, a, 0:N])], [K("gxw", l), K("xc_b", a)])
                gx = tmpf()
                act(gx[:, 0:N], ps_gx[:, 0:N], AF.Sigmoid, [K("ps", ps_gx.name), kv], [K(gx.name)], bias=V(l, "gxb", a))
                ps_ga = next_ps()
                mm(ps_ga, N, [(gaw[l][:, a, :], xcb3[:, a, 0:N])], [K("gaw", l), K("xc_b", a)])
                at = tmpf()
                act(at[:, 0:N], ps_ga[:, 0:N], AF.Sigmoid, [K("ps", ps_ga.name), kv], [K(at.name)], bias=V(l, "gab", a))
                act(at[:, 0:N], at[:, 0:N], AF.Exp, [K(at.name), kv], [K(at.name)], scale=V(l, "lc", a))
                ml = tmpf()
                tt(ml[:, 0:N], at[:, 0:N], at[:, 0:N], ALU.mult, [K(at.name)], [K(ml.name)], eng=POOLX)
                act(ml[:, 0:N], ml[:, 0:N], AF.Sqrt, [K(ml.name)], [K(ml.name)], bias=ONE, scale=-1.0)
                tt(gx[:, 0:N], gx[:, 0:N], xc3[:, a, 0:N], ALU.mult, [K(gx.name), kxc], [K(gx.name)], eng=POOLX)
                tt(gx[:, 0:N], gx[:, 0:N], ml[:, 0:N], ALU.mult, [K(gx.name), K(ml.name)], [K(gx.name)])
                h = tmpf()
                kh = K(h.name)
                if prompt:
                    S.add("dve", lambda e, h=h, at=at, gx=gx, a=a: e.tensor_tensor_scan(
                        out=h[:, 0:N], data0=at[:, 0:N], data1=gx[:, 0:N], initial=lru_h[l][:, a:a + 1], op0=ALU.mult, op1=ALU.add),
                        [K(at.name), K(gx.name), K("lru_h", l)], [kh], dur=0.12 + 2 * N / 960.0)
                    cp(lru_h[l][:, a:a + 1], h[:, N - 1:N], [kh], [K("lru_h", l)])
                    if last:
                        dma("sp", o_h_p[l].rearrange("(a p) -> p a", p=128)[:, a:a + 1], h[:, N - 1:N], [kh], (), output=True)
                else:
                    for t in range(L):
                        prev = s_h0[:, a, :] if t == 0 else h[:, t - 1:N:L]
                        tt(h[:, t:N:L], at[:, t:N:L], prev, ALU.mult, [K(at.name), K("s_h0"), kh], [kh])
                        tt(h[:, t:N:L], h[:, t:N:L], gx[:, t:N:L], ALU.add, [kh, K(gx.name)], [kh])
                    cp(stg_o[:, 2, a:64:4], h[:, L - 1:N:L], [kh], [K("stg_o")])
                ps_ya = next_ps()
                mm(ps_ya, N, [(wya[:, kt, a * 128:(a + 1) * 128], x_b[:, kt, 0:N]) for kt in range(8)], [kwya] + kxb)
                gl = tmpf()
                act(gl[:, 0:N], ps_ya[:, 0:N], AF.Gelu_apprx_tanh, [K("ps", ps_ya.name)], [K(gl.name)])
                tt(v_b[:, a, 0:N], h[:, 0:N], gl[:, 0:N], ALU.mult, [kh, K(gl.name)], [K("v_b")], eng=POOLX)
                yield
            loadsB = []
            for j in range(3):
                w_ = yield ("load", w_rows(w_in[l], 0, D, 1024 + 512 * j, 512), 8, 512)
                loadsB.append(w_)
            for _ in proj_merge_gen(l, N, 0, K("v_b"), loads_mA):
                yield
            (wsb, kwsb), (wsc, kwsc), (wsh, kwsh) = loadsB
            if sample:
                cp(upB[:, :, :, 0:2], s_sc0[:, :, :, :], [K("s_sc0")], [kup, K("r_f"), K("r_b")])
            loads_mB = yield from merge_loads_g(l, 1, proj_b)
            for a in range(4):
                if prompt:
                    cp(upB[:, a, 0, 0:2], convB_st[l][:, a, :], [K("convB_st", l)], [kup], eng="act")
                ps_c = next_ps()
                mm(ps_c, N, [(wsc[:, kt, a * 128:(a + 1) * 128], x_b[:, kt, 0:N]) for kt in range(8)], [kwsc] + kxb)
                sct = tmpf()
                cp(sct[:, 0:N], ps_c[:, 0:N], [K("ps", ps_c.name)], [K(sct.name)], eng="act")
                ps_h = next_ps()
                mm(ps_h, N, [(wsh[:, kt, a * 128:(a + 1) * 128], x_b[:, kt, 0:N]) for kt in range(8)], [kwsh] + kxb)
                tt(upB[:, a, :, 2:2 + L], seqv(ps_h[:, 0:N]), seqv(sct[:, 0:N]), ALU.mult, [K("ps", ps_h.name), K(sct.name)], [kup])
                cu = tmpf()
                cuo = seqv(cu[:, 0:N])
                act(cuo, upB[:, a, :, 0:L], AF.Identity, [kup, kv], [K(cu.name)], scale=V(l, "cbw0", a))
                for k in range(1, 3):
                    stt(cuo, upB[:, a, :, k:k + L], V(l, "cbw%d" % k, a), cuo, ALU.mult, ALU.add, [kup, kv, K(cu.name)], [K(cu.name)])
                if prompt:
                    cp(convB_st[l][:, a, :], upB[:, a, 0, L:L + 2], [kup], [K("convB_st", l)], eng="act")
                    if last:
                        dma("sp", o_sc_p[l].rearrange("k (a p) -> p a k", p=128)[:, a], upB[:, a, 0, L:L + 2], [kup], (), output=True)
                else:
                    cp(stg_o[:, 3, 0:128].rearrange("p (s k a) -> p s k a", k=2, a=4)[:, :, :, a], upB[:, a, :, L:L + 2], [kup], [K("stg_o")])
                ps_b = next_ps()
                mm(ps_b, N, [(wsb[:, kt, a * 128:(a + 1) * 128], x_b[:, kt, 0:N]) for kt in range(8)], [kwsb] + kxb)
                tt(v_b[:, a, 0:N], ps_b[:, 0:N], cu[:, 0:N], ALU.mult, [K("ps", ps_b.name), K(cu.name)], [K("v_b")])
                yield
            post["glu"] = yield ("load", w_rows(glu_w[l], 0, 512, 0, 512), 4, 512)
            post["mC"] = yield from merge_loads_g(l, 2, proj_c)
            for _ in proj_merge_gen(l, N, 1, K("v_b"), loads_mB):
                yield

        units = gen_units()
        units_alive = [True]

        def step_units():
            try:
                r = next(units)
                while r is not None:
                    v = yield r
                    r = units.send(v)
            except StopIteration:
                units_alive[0] = False
        cur = s5_B(0)
        for i in range(16):
            hb = s5_body(i, *cur)
            s5_C(i, *hb)
            if i % 4 == 3:
                s5_epi(i // 4)
            if i >= 1 and units_alive[0]:
                yield from step_units()
            cur = s5_B(i + 1) if i + 1 < 16 else None
        while units_alive[0]:
            yield from step_units()
        if prompt and last:
            kall = [K("ssm_h", l, i) for i in range(16)]
            dma("sp", o_re_p[l].rearrange("(i q) -> q i", q=128), ssm_h[l][:, :, 0], kall, (), output=True)
            dma("sp", o_im_p[l].rearrange("(i q) -> q i", q=128), ssm_h[l][:, :, 1], kall, (), output=True)
        if sample:
            nrows = [120, 72, 64, 128, 128, 128, 128, 128]
            dsts = [o_conv_s[l].rearrange("s k (a p) -> (s k a) p", p=128)[0:120], o_conv_s[l].rearrange("s k (a p) -> (s k a) p", p=128)[120:192],
                    o_h_s[l].rearrange("s (a p) -> (s a) p", p=128), o_sc_s[l].rearrange("s k (a p) -> (s k a) p", p=128),
                    o_re_s[l].rearrange("s (i q) -> (s i) q", q=128)[0:128], o_re_s[l].rearrange("s (i q) -> (s i) q", q=128)[128:256],
                    o_im_s[l].rearrange("s (i q) -> (s i) q", q=128)[0:128], o_im_s[l].rearrange("s (i q) -> (s i) q", q=128)[128:256]]
            for j in range(8):
                nr = nrows[j]
                ps = next_ps()
                S.add("pe", lambda e, ps=ps, j=j, nr=nr: e.transpose(out=ps[0:nr, 0:128], in_=stg_o[:, j, 0:nr], identity=ident[:, :]),
                      [K("stg_o"), K("ident")], [K("ps", ps.name)])
                cp(out_rows[0:nr, j, :], ps[0:nr, 0:128], [K("ps", ps.name)], [K("sst")], eng="act")
                dma("sp", dsts[j], out_rows[0:nr, j, :], [K("sst")], (), output=True)
        chk('C')
        wgu, kwgu = post["glu"]
        for a in range(4):
            ps = next_ps()
            mm(ps, N, [(wgu[:, kt, a * 128:(a + 1) * 128], zc_t[:, kt, 0:N]) for kt in range(4)], [kwgu] + [K("zc", kt) for kt in range(4)])
            sg = tmpf()
            act(sg[:, 0:N], ps[:, 0:N], AF.Sigmoid, [K("ps", ps.name), kv], [K(sg.name)], bias=V(l, "glb", a))
            tt(v_b[:, a, 0:N], zc_t[:, a, 0:N], sg[:, 0:N], ALU.mult, [K("zc", a), K(sg.name)], [K("v_b")])
        for _ in proj_merge_gen(l, N, 2, K("v_b"), post["mC"]):
            pass
        chk('mC')

        r3 = r_f.rearrange("p (f n) -> p f n", f=8)
        kr = K("r_f")
        alias_keys = [kup] + [K("xc_f", a) for a in range(4)] + [K("us_f", a) for a in range(4)] + \
                     [K("xc_b", a) for a in range(4)] + [K("us_b", a) for a in range(4)]
        for half in range(2):
            wo, kwo = yield ("load", w_rows(w_out[l], 0, D, 512 * half, 512), 8, 512)
            for ff in range(4):
                f = 4 * half + ff
                ps = next_ps()
                mm(ps, N, [(wo[:, kt, ff * 128:(ff + 1) * 128], m_b[:, kt, 0:N]) for kt in range(8)], [kwo] + [K("m_b", kt) for kt in range(8)])
                stt(r3[:, f, 0:N], x_f[:, f, 0:N], 2.0 * ALPHA, ps[:, 0:N], ALU.mult, ALU.add, [K("x_f", f), K("ps", ps.name)], [kr] + alias_keys)
        chk('W')
        layer_norm(l, N, "l1g", "l1b", eps_ap=EPS4)
        chk('LN1')

        for half in range(2):
            for q in range(4):
                wu, kwu = yield ("load", w_rows(mlp_up[l], 0, D, 2048 * half + 512 * q, 512), 8, 512)
                for jj in range(4):
                    j = 4 * q + jj
                    ps = next_ps()
                    mm(ps, N, [(wu[:, kt, jj * 128:(jj + 1) * 128], x_b[:, kt, 0:N]) for kt in range(8)], [kwu] + kxb)
                    rl = tmpf()
                    act(rl[:, 0:N], ps[:, 0:N], AF.Relu, [K("ps", ps.name)], [K(rl.name)])
                    act(hid_v[:, j, 0:N], rl[:, 0:N], AF.Square, [K(rl.name)], [K("hid")])
            for ch in range(2):
                wd0, kwd0 = yield ("load", w_rows(mlp_down[l], 2048 * half, 1024, 512 * ch, 512), 8, 512)
                wd1, kwd1 = yield ("load", w_rows(mlp_down[l], 2048 * half + 1024, 1024, 512 * ch, 512), 8, 512)
                for ff in range(4):
                    f = 4 * ch + ff
                    ps = next_ps()
                    pairs = [(wd0[:, kt, ff * 128:(ff + 1) * 128], hid_v[:, kt, 0:N]) for kt in range(8)] + \
                            [(wd1[:, kt, ff * 128:(ff + 1) * 128], hid_v[:, 8 + kt, 0:N]) for kt in range(8)]
                    mm(ps, N, pairs, [kwd0, kwd1, K("hid")])
                    if half == 0:
                        stt(r3[:, f, 0:N], x_f[:, f, 0:N], ALPHA, ps[:, 0:N], ALU.mult, ALU.add, [K("x_f", f), K("ps", ps.name)], [kr])
                    else:
                        tt(r3[:, f, 0:N], r3[:, f, 0:N], ps[:, 0:N], ALU.add, [kr, K("ps", ps.name)], [kr])
        chk('M')
        layer_norm(l, N, "l2g", "l2b")

    def load_x(C, src, t0, N):
        activate(C)
        ntt = (N + 127) // 128
        if ntt == 4:
            stg = [(us_f[:, 0:1024], [K("us_f", 0), K("us_f", 1)]), (us_f[:, 1024:2048], [K("us_f", 2), K("us_f", 3)]),
                   (us_b[:, :].bitcast(F32), [K("us_b", a_) for a_ in range(4)]),
                   (zc_t[:, :, :].rearrange("p a n -> p (a n)").bitcast(F32), [K("zc", a_) for a_ in range(4)])]
            for tk in range(4):
                dma("sp", stg[tk][0], src[t0 + 128 * tk:t0 + 128 * tk + 128, :], (), stg[tk][1])
            for f in range(8):
                ps = next_ps()
                for tk in range(4):
                    S.add("pe", lambda e, ps=ps, tk=tk, f=f, st=stg[tk][0]: e.transpose(out=ps[:, 128 * tk:128 * tk + 128], in_=st[:, 128 * f:128 * f + 128],
                                                                                 identity=ident[:, :]),
                          stg[tk][1] + [K("ident")], [K("ps", ps.name)])
                cp(x_b[:, f, 0:N], ps[:, 0:N], [K("ps", ps.name)], [K("x_b", f)], eng="dve")
        xin3 = xin.rearrange("p (t d) -> p t d", d=D)
        for tk in range(ntt):
            nt = min(128, N - 128 * tk)
            dma("sp", xin3[0:nt, tk, :], src[t0 + 128 * tk:t0 + 128 * tk + nt, :], (), [K("xin")])
        for f in range(8):
            ps = next_ps()
            for tk in range(ntt):
                nt = min(128, N - 128 * tk)
                S.add("pe", lambda e, ps=ps, tk=tk, nt=nt, f=f, xin3=xin3: e.transpose(out=ps[:, 128 * tk:128 * tk + nt], in_=xin3[0:nt, tk, 128 * f:128 * f + 128],
                                                                                identity=ident[0:nt, 0:nt]),
                      [K("xin"), K("ident")], [K("ps", ps.name)])
            cp(x_f[:, f, 0:N], ps[:, 0:N], [K("ps", ps.name)], [K("x_f", f)], eng="act")
            if ntt != 4:
                cp(x_b[:, f, 0:N], ps[:, 0:N], [K("ps", ps.name)], [K("x_b", f)], eng="dve")

    def store_y(C, dst, t0, N):
        activate(C)
        yo3 = yout.rearrange("p (t d) -> p t d", d=D)
        ntt = (N + 127) // 128
        for tk in range(ntt):
            nt = min(128, N - 128 * tk)
            for hh in range(2):
                ps = next_ps()
                for ff in range(4):
                    f = 4 * hh + ff
                    S.add("pe", lambda e, ps=ps, tk=tk, nt=nt, f=f, ff=ff, xf=x_f: e.transpose(out=ps[0:nt, 128 * ff:128 * ff + 128], in_=xf[:, f, 128 * tk:128 * tk + nt],
                                                                                       identity=ident[:, :]),
                          [K("x_f", f), K("ident")], [K("ps", ps.name)])
                cp(yo3[0:nt, tk, 512 * hh:512 * hh + 512], ps[0:nt, 0:512], [K("ps", ps.name)], [K("yout")], eng="act")
            dma("sp", dst[t0 + 128 * tk:t0 + 128 * tk + nt, :], yo3[0:nt, tk, :], [K("yout")], (), output=True)

    def run_layer(l, ctxs, last):
        gens = []
        for (C, N, nseq, L, prompt) in ctxs:
            activate(C)
            gens.append(layer_pass(l, N, nseq, L, prompt, last and prompt))
        reqs = []
        for (C, *_), g in zip(ctxs, gens):
            activate(C)
            reqs.append(next(g, None))
        while any(r is not None for r in reqs):
            r0 = [r for r in reqs if r is not None][0]
            view = load_w(r0[1], r0[2], r0[3])
            new = []
            for (C, *_), g, r in zip(ctxs, gens, reqs):
                if r is None:
                    new.append(None)
                    continue
                activate(C)
                try:
                    new.append(g.send(view))
                except StopIteration:
                    new.append(None)
            reqs = new

    fold = dbg.get('fold', True)
    SCTX = (CS, NSS * DL, NSS, DL, False)
    passes = [(512 * c, [(CP, 512, 1, 512, True)] + ([SCTX] if (fold and c == 0) else [])) for c in range(4)]
    if not fold:
        passes.append((0, [SCTX]))
    if 'passes' in dbg:
        passes = [passes[i] for i in dbg['passes']]
    try:
        for pi, (t0, ctxs) in enumerate(passes):
            ps_nrot[0] = 6 if any(not c_[4] for c_ in ctxs) else 7
            if pi >= 1 and fold and dbg.get('xtemps', True) and len(tf_pool) == NT:
                for ap_, nm_, keys_ in tf_extra:
                    AL[(nm_,)] = keys_
                    tf_pool.append(_TV(ap_, nm_))
                tf_n[0] = len(tf_pool)
            for (C, N, nseq, L, prompt) in ctxs:
                load_x(C, xp if prompt else xs, t0 if prompt else 0, N)
            chk('xin')
            for l in range(dbg.get('layers', DEPTH)):
                if pi == 0 and l == 1:
                    prep_tables(1)
                run_layer(l, ctxs, last=(pi == dbg.get('last_pass', 3)))
            for (C, N, nseq, L, prompt) in ctxs:
                store_y(C, yp if prompt else ys, t0 if prompt else 0, N)
        activate(CP)

    except _Stop:
        activate(CP)
        ypv = yp.rearrange("(p a) d -> p (a d)", p=128)
        dma("sp", ypv[:, 0:4096], m_f[:, :, :].rearrange("p f n -> p (f n)"), [K("m_f", f) for f in range(8)], (), output=True)
        dma("sp", ypv[:, 4096:8192], x_f[:, :, :].rearrange("p f n -> p (f n)"), [K("x_f", f) for f in range(8)], (), output=True)
        dma("sp", ypv[:, 8192:12288], arena[:, 0:4096], [K("r_f")], (), output=True)
    S.emit(es)
    if dbg.get('verbose'):
        print('ops', len(S.all), 'est makespan us', S.makespan)
    es.close()
    return nc


_CACHE = {}


def kernel(**inputs):
    f32 = lambda a: np.ascontiguousarray(np.asarray(a, dtype=np.float32))
    inp = {k: f32(v) for k, v in inputs.items()}
    if "nc" not in _CACHE:
        _CACHE["nc"] = build_program()
    nc = _CACHE["nc"]
    shared = {}
    for k in ("w_in", "conv_a_w", "conv_a_b", "gate_x_w", "gate_x_b", "gate_a_w", "gate_a_b", "lru_lambda", "conv_b_w",
              "ssm_d", "glu_w", "glu_b", "proj_a", "proj_b", "proj_c", "w_out", "ln1_g", "ln1_b", "mlp_up", "mlp_down", "ln2_g", "ln2_b"):
        shared[k] = inp[k]
    for k in ("ssm_a_re", "ssm_a_im", "ssm_log_dt"):
        shared[k] = inp[k].reshape(DEPTH, 2048)
    for k in ("ssm_b_re", "ssm_b_im"):
        shared[k] = inp[k].reshape(DEPTH, 2048 * 16)
    for k in ("ssm_c_re", "ssm_c_im"):
        shared[k] = inp[k].reshape(DEPTH, 512 * 64)
    in_maps = []
    for c in range(NCORES):
        sl = slice(NSS * c, NSS * (c + 1))
        m = dict(shared)
        m["xp"] = np.ascontiguousarray(inp["x_prompt"][c])
        m["xs"] = np.ascontiguousarray(inp["x_sample"][sl].reshape(NSS * DL, D))
        m["st_conv"] = np.ascontiguousarray(inp["state_lru_conv"][:, sl])
        m["st_h"] = np.ascontiguousarray(inp["state_lru_h"][:, sl])
        m["st_sc"] = np.ascontiguousarray(inp["state_sconv"][:, sl])
        m["st_re"] = np.ascontiguousarray(inp["state_ssm_re"][:, sl].reshape(DEPTH, NSS, 2048))
        m["st_im"] = np.ascontiguousarray(inp["state_ssm_im"][:, sl].reshape(DEPTH, NSS, 2048))
        in_maps.append(m)
    res = run_bass_kernel_spmd(nc, in_maps, core_ids=list(range(NCORES)))
    R = res.results
    y_prompt = np.stack([R[c]["yp"] for c in range(NCORES)], 0)
    y_sample = np.concatenate([R[c]["ys"].reshape(NSS, DL, D) for c in range(NCORES)], 0)
    stk = lambda name, shp: np.stack([R[c][name].reshape(shp) for c in range(NCORES)], 1)
    cat = lambda name, shp: np.concatenate([R[c][name].reshape(shp) for c in range(NCORES)], 1)
    outs = (y_prompt, y_sample,
            stk("o_conv_p", (DEPTH, 3, 512)), stk("o_h_p", (DEPTH, 512)), stk("o_sc_p", (DEPTH, 2, 512)),
            stk("o_re_p", (DEPTH, G, 64)), stk("o_im_p", (DEPTH, G, 64)),
            cat("o_conv_s", (DEPTH, NSS, 3, 512)), cat("o_h_s", (DEPTH, NSS, 512)), cat("o_sc_s", (DEPTH, NSS, 2, 512)),
            cat("o_re_s", (DEPTH, NSS, G, 64)), cat("o_im_s", (DEPTH, NSS, G, 64)))
    return tuple(np.ascontiguousarray(o, dtype=np.float32) for o in outs)
```
